# Optimizing a Trainium2 kernel written in Bass

```python
import math
import jax
import jax.numpy as jnp
from jax import lax
import numpy as np

D_MODEL = 2048
BATCH = 1
SEQ = 16384
DEPTH = 2

EPS = 1e-6
MIX_WIDTH = D_MODEL
S5_WIDTH = MIX_WIDTH // 2
S5_GROUP = 16
S5_GROUPS = S5_WIDTH // S5_GROUP
S5_STATE = 64
S5_DT_MIN = 1e-3
S5_DT_MAX = 1e-1
GDN_HEAD_DIM = 128
GDN_V_HEADS = (MIX_WIDTH - S5_WIDTH) // GDN_HEAD_DIM
GDN_QK_HEADS = GDN_V_HEADS // 2
GDN_QK_W = GDN_QK_HEADS * GDN_HEAD_DIM
GDN_V_W = GDN_V_HEADS * GDN_HEAD_DIM
GDN_CONV = 4
GDN_CHUNK = 64
GLA_HEADS = 4
GLA_DK = MIX_WIDTH // 2
GLA_DV = MIX_WIDTH
GLA_LOWRANK = 16
GLA_TAU = 16.0
GLA_CHUNK = 64
FFN_HIDDEN = ((8 * D_MODEL // 3 + 255) // 256) * 256
IN0_SIZES = (S5_WIDTH, GDN_QK_W, GDN_QK_W, GDN_V_W, GDN_V_W, GDN_V_HEADS, GDN_V_HEADS)
IN1_SIZES = (GLA_DK, GLA_DK, GLA_DV, GLA_DV, GLA_LOWRANK)

kernel_name = "hybrid_s5_gdn_gla_sandwich_adaln"


def _split(y, sizes):
    points = np.cumsum(np.array(sizes))[:-1].tolist()
    return jnp.split(y, points, axis=-1)


def rmsnorm(x, w):
    xf = x.astype(jnp.float32)
    y = xf * lax.rsqrt(jnp.mean(xf * xf, axis=-1, keepdims=True) + EPS)
    return (y * w.astype(jnp.float32)).astype(x.dtype)


def l2norm(x):
    return x * lax.rsqrt(jnp.sum(x * x, axis=-1, keepdims=True) + EPS)


def ada_modulation(c, w, b):
    mod = jax.nn.silu(c) @ w + b
    return jnp.split(mod[:, None, :], 6, axis=-1)


def swiglu(h, w_gate, w_up, w_down):
    return (jax.nn.silu(h @ w_gate) * (h @ w_up)) @ w_down


def causal_dwconv(x, w):
    k = w.shape[0]
    return lax.conv_general_dilated(
        x, w[:, None, :].astype(x.dtype), window_strides=(1,), padding=((k - 1, 0),),
        dimension_numbers=("NWC", "WIO", "NWC"), feature_group_count=x.shape[-1])


def to_chunks(x, c):
    b, l, h = x.shape[:3]
    x = x.reshape((b, l // c, c, h) + x.shape[3:])
    return jnp.moveaxis(x, 3, 1)


def s5_mixer(u, lam_re, lam_im, log_step, b_re, b_im, c_re, c_im, d, glu_w, glu_b):
    f32 = jnp.float32
    bsz, l, _ = u.shape
    uf = u.astype(f32).reshape(bsz, l, S5_GROUPS, S5_GROUP)
    lr, li = lam_re.astype(f32), lam_im.astype(f32)
    dt = jnp.exp(log_step.astype(f32))[:, None]
    mag = jnp.exp(lr * dt)
    ab_re, ab_im = mag * jnp.cos(li * dt), mag * jnp.sin(li * dt)
    den = lr * lr + li * li
    nr, ni = ab_re - 1.0, ab_im
    f_re = (nr * lr + ni * li) / den
    f_im = (ni * lr - nr * li) / den
    br, bi = b_re.astype(f32), b_im.astype(f32)
    bb_re = f_re[..., None] * br - f_im[..., None] * bi
    bb_im = f_re[..., None] * bi + f_im[..., None] * br
    bu_re = jnp.einsum("blgp,gnp->blgn", uf, bb_re)
    bu_im = jnp.einsum("blgp,gnp->blgn", uf, bb_im)
    a_re = jnp.broadcast_to(ab_re, bu_re.shape)
    a_im = jnp.broadcast_to(ab_im, bu_im.shape)

    def combine(e1, e2):
        a1r, a1i, b1r, b1i = e1
        a2r, a2i, b2r, b2i = e2
        return (a2r * a1r - a2i * a1i, a2r * a1i + a2i * a1r,
                a2r * b1r - a2i * b1i + b2r, a2r * b1i + a2i * b1r + b2i)

    _, _, xr, xi = lax.associative_scan(combine, (a_re, a_im, bu_re, bu_im), axis=1)
    y = (jnp.einsum("blgn,gpn->blgp", xr, c_re.astype(f32))
         - jnp.einsum("blgn,gpn->blgp", xi, c_im.astype(f32))
         + d.astype(f32) * uf)
    y = jax.nn.gelu(y.reshape(bsz, l, S5_WIDTH)).astype(u.dtype)
    return y * jax.nn.sigmoid(y @ glu_w + glu_b)


def gated_delta_chunked(q, k, v, g, beta):
    bsz, l, h, _ = q.shape
    dv = v.shape[-1]
    cs = GDN_CHUNK
    q, k, v = to_chunks(q, cs), to_chunks(k, cs), to_chunks(v, cs)
    g = jnp.cumsum(to_chunks(g, cs), axis=-1)
    beta = to_chunks(beta, cs)
    causal = jnp.tril(jnp.ones((cs, cs), bool))
    strict = jnp.tril(jnp.ones((cs, cs), bool), -1)
    decay = jnp.exp(jnp.where(causal, g[..., :, None] - g[..., None, :], -jnp.inf))
    kb = k * beta[..., None]
    lower = jnp.where(strict, jnp.einsum("bhnid,bhnjd->bhnij", kb, k) * decay, 0.0)
    eye = jnp.eye(cs, dtype=q.dtype)
    t_mat = lax.linalg.triangular_solve(lower + eye, jnp.broadcast_to(eye, lower.shape),
                                        left_side=True, lower=True, unit_diagonal=True)
    u_vals = t_mat @ (v * beta[..., None])
    w_keys = t_mat @ (kb * jnp.exp(g)[..., None])
    attn = jnp.where(causal, jnp.einsum("bhnid,bhnjd->bhnij", q, k) * decay, 0.0)
    q_dec = q * jnp.exp(g)[..., None]
    k_dec = k * jnp.exp(g[..., -1:] - g)[..., None]
    g_last = jnp.exp(g[..., -1])
    xs = tuple(jnp.moveaxis(t, 2, 0) for t in (u_vals, w_keys, attn, q_dec, k_dec, g_last))

    def step(s, inp):
        u_c, w_c, a_c, qd, kd, gl = inp
        v_new = u_c - jnp.einsum("bhck,bhkv->bhcv", w_c, s)
        o_c = jnp.einsum("bhck,bhkv->bhcv", qd, s) + jnp.einsum("bhij,bhjv->bhiv", a_c, v_new)
        s = s * gl[..., None, None] + jnp.einsum("bhck,bhcv->bhkv", kd, v_new)
        return s, o_c

    s0 = jnp.zeros((bsz, h, q.shape[-1], dv), q.dtype)
    _, o = lax.scan(step, s0, xs)
    o = jnp.moveaxis(o, 0, 2)
    return o.transpose(0, 2, 3, 1, 4).reshape(bsz, l, h, dv)


def gdn_mixer(q, k, v, z, a, b, conv_w, a_log, dt_bias, norm_w):
    f32 = jnp.float32
    bsz, l, _ = q.shape
    qkv = jax.nn.silu(causal_dwconv(jnp.concatenate([q, k, v], axis=-1), conv_w)).astype(f32)
    q, k, v = _split(qkv, (GDN_QK_W, GDN_QK_W, GDN_V_W))
    rep = GDN_V_HEADS // GDN_QK_HEADS
    q = jnp.repeat(l2norm(q.reshape(bsz, l, GDN_QK_HEADS, GDN_HEAD_DIM)), rep, axis=2) * GDN_HEAD_DIM ** -0.5
    k = jnp.repeat(l2norm(k.reshape(bsz, l, GDN_QK_HEADS, GDN_HEAD_DIM)), rep, axis=2)
    v = v.reshape(bsz, l, GDN_V_HEADS, GDN_HEAD_DIM)
    beta = jax.nn.sigmoid(b.astype(f32))
    g = -jnp.exp(a_log.astype(f32)) * jax.nn.softplus(a.astype(f32) + dt_bias.astype(f32))
    o = gated_delta_chunked(q, k, v, g, beta)
    o = rmsnorm(o, norm_w) * jax.nn.silu(z.astype(f32).reshape(bsz, l, GDN_V_HEADS, GDN_HEAD_DIM))
    return o.reshape(bsz, l, GDN_V_W).astype(z.dtype)


def gla_chunked(q, k, v, gk):
    bsz, l, h, dk = q.shape
    dv = v.shape[-1]
    cs = GLA_CHUNK
    q, k, v, gk = (to_chunks(t, cs) for t in (q, k, v, gk))
    bcum = jnp.cumsum(gk, axis=3)
    q_t = q * jnp.exp(bcum)
    k_t = k * jnp.exp(-bcum)
    causal = jnp.tril(jnp.ones((cs, cs), bool))
    attn = jnp.where(causal, jnp.einsum("bhnik,bhnjk->bhnij", q_t, k_t), 0.0)
    o_intra = attn @ v
    k_dec = k * jnp.exp(bcum[..., -1:, :] - bcum)
    g_last = jnp.exp(bcum[..., -1, :])
    xs = tuple(jnp.moveaxis(t, 2, 0) for t in (q_t, k_dec, v, g_last))

    def step(s, inp):
        qt, kd, vc, gl = inp
        o_c = jnp.einsum("bhck,bhkv->bhcv", qt, s)
        s = s * gl[..., None] + jnp.einsum("bhck,bhcv->bhkv", kd, vc)
        return s, o_c

    s0 = jnp.zeros((bsz, h, dk, dv), q.dtype)
    _, o_inter = lax.scan(step, s0, xs)
    o = o_intra + jnp.moveaxis(o_inter, 0, 2)
    return o.transpose(0, 2, 3, 1, 4).reshape(bsz, l, h, dv)


def gla_mixer(q, k, v, r, g_low, gate_w2, gate_b, norm_w):
    f32 = jnp.float32
    bsz, l, _ = q.shape
    dk, dv = GLA_DK // GLA_HEADS, GLA_DV // GLA_HEADS
    gk = jax.nn.log_sigmoid((g_low @ gate_w2 + gate_b).astype(f32)) / GLA_TAU
    q = q.astype(f32).reshape(bsz, l, GLA_HEADS, dk) * dk ** -0.5
    k = k.astype(f32).reshape(bsz, l, GLA_HEADS, dk)
    v = v.astype(f32).reshape(bsz, l, GLA_HEADS, dv)
    gk = gk.reshape(bsz, l, GLA_HEADS, dk)
    o = gla_chunked(q, k, v, gk)
    o = rmsnorm(o, norm_w) * jax.nn.silu(r.astype(f32).reshape(bsz, l, GLA_HEADS, dv))
    return o.reshape(bsz, l, GLA_DV).astype(r.dtype)


def setup_inputs(seed: int = 0) -> dict:
    key = jax.random.key(seed)
    ks = iter(jax.random.split(key, 64))
    f32 = jnp.float32
    D = D_MODEL

    def nrm(shape, scale):
        return jax.random.normal(next(ks), shape, f32) * scale

    def unif(shape, lo, hi):
        return jax.random.uniform(next(ks), shape, f32, lo, hi)

    def gain(n):
        return 1.0 + nrm((n,), 0.02)

    inp = {}
    inp["x"] = nrm((BATCH, SEQ, D), 1.0)
    inp["c"] = nrm((BATCH, D), 1.0)
    inp["ada_w0"] = nrm((D, 6 * D), D ** -0.5)
    inp["ada_b0"] = nrm((6 * D,), 0.02)
    inp["mix_pre0"] = gain(D)
    inp["mix_post0"] = gain(D)
    inp["ffn_pre0"] = gain(D)
    inp["ffn_post0"] = gain(D)
    inp["w_in0"] = nrm((D, sum(IN0_SIZES)), D ** -0.5)
    n_idx = jnp.arange(S5_STATE, dtype=f32)[None, :]
    inp["s5_lambda_re"] = -0.5 + nrm((S5_GROUPS, S5_STATE), 0.01)
    inp["s5_lambda_im"] = math.pi * n_idx + nrm((S5_GROUPS, S5_STATE), 0.01)
    inp["s5_log_step"] = unif((S5_GROUPS,), math.log(S5_DT_MIN), math.log(S5_DT_MAX))
    inp["s5_b_re"] = nrm((S5_GROUPS, S5_STATE, S5_GROUP), (2 * S5_GROUP) ** -0.5)
    inp["s5_b_im"] = nrm((S5_GROUPS, S5_STATE, S5_GROUP), (2 * S5_GROUP) ** -0.5)
    inp["s5_c_re"] = nrm((S5_GROUPS, S5_GROUP, S5_STATE), (2 * S5_STATE) ** -0.5)
    inp["s5_c_im"] = nrm((S5_GROUPS, S5_GROUP, S5_STATE), (2 * S5_STATE) ** -0.5)
    inp["s5_d"] = nrm((S5_GROUPS, S5_GROUP), 0.5)
    inp["s5_glu_w"] = nrm((S5_WIDTH, S5_WIDTH), S5_WIDTH ** -0.5)
    inp["s5_glu_b"] = nrm((S5_WIDTH,), 0.02)
    inp["gdn_conv_w"] = nrm((GDN_CONV, 2 * GDN_QK_W + GDN_V_W), GDN_CONV ** -0.5)
    inp["gdn_a_log"] = jnp.log(unif((GDN_V_HEADS,), 1.0, 16.0))
    dt = jnp.exp(unif((GDN_V_HEADS,), math.log(1e-3), math.log(1e-1)))
    inp["gdn_dt_bias"] = dt + jnp.log(-jnp.expm1(-dt))
    inp["gdn_norm_w"] = gain(GDN_HEAD_DIM)
    inp["w_out0"] = nrm((MIX_WIDTH, D), MIX_WIDTH ** -0.5)
    inp["ffn_gate0"] = nrm((D, FFN_HIDDEN), D ** -0.5)
    inp["ffn_up0"] = nrm((D, FFN_HIDDEN), D ** -0.5)
    inp["ffn_down0"] = nrm((FFN_HIDDEN, D), FFN_HIDDEN ** -0.5)
    inp["ada_w1"] = nrm((D, 6 * D), D ** -0.5)
    inp["ada_b1"] = nrm((6 * D,), 0.02)
    inp["mix_pre1"] = gain(D)
    inp["mix_post1"] = gain(D)
    inp["ffn_pre1"] = gain(D)
    inp["ffn_post1"] = gain(D)
    inp["w_in1"] = nrm((D, sum(IN1_SIZES)), D ** -0.5)
    inp["gla_gate_w2"] = nrm((GLA_LOWRANK, GLA_DK), GLA_LOWRANK ** -0.5)
    inp["gla_gate_b"] = nrm((GLA_DK,), 0.1)
    inp["gla_norm_w"] = gain(GLA_DV // GLA_HEADS)
    inp["w_out1"] = nrm((GLA_DV, D), GLA_DV ** -0.5)
    inp["ffn_gate1"] = nrm((D, FFN_HIDDEN), D ** -0.5)
    inp["ffn_up1"] = nrm((D, FFN_HIDDEN), D ** -0.5)
    inp["ffn_down1"] = nrm((FFN_HIDDEN, D), FFN_HIDDEN ** -0.5)
    return inp


def reference(x, c, ada_w0, ada_b0, mix_pre0, mix_post0, ffn_pre0, ffn_post0, w_in0,
              s5_lambda_re, s5_lambda_im, s5_log_step, s5_b_re, s5_b_im, s5_c_re, s5_c_im,
              s5_d, s5_glu_w, s5_glu_b, gdn_conv_w, gdn_a_log, gdn_dt_bias, gdn_norm_w,
              w_out0, ffn_gate0, ffn_up0, ffn_down0,
              ada_w1, ada_b1, mix_pre1, mix_post1, ffn_pre1, ffn_post1, w_in1,
              gla_gate_w2, gla_gate_b, gla_norm_w, w_out1, ffn_gate1, ffn_up1, ffn_down1):

    def mixer_even(h):
        u, q, k, v, z, a, b = _split(h @ w_in0, IN0_SIZES)
        y_a = s5_mixer(u, s5_lambda_re, s5_lambda_im, s5_log_step, s5_b_re, s5_b_im,
                       s5_c_re, s5_c_im, s5_d, s5_glu_w, s5_glu_b)
        y_b = gdn_mixer(q, k, v, z, a, b, gdn_conv_w, gdn_a_log, gdn_dt_bias, gdn_norm_w)
        return jnp.concatenate([y_a, y_b.astype(y_a.dtype)], axis=-1) @ w_out0

    def mixer_odd(h):
        q, k, v, r, g_low = _split(h @ w_in1, IN1_SIZES)
        return gla_mixer(q, k, v, r, g_low, gla_gate_w2, gla_gate_b, gla_norm_w) @ w_out1

    layers = (
        (mixer_even, ada_w0, ada_b0, mix_pre0, mix_post0, ffn_pre0, ffn_post0, ffn_gate0, ffn_up0, ffn_down0),
        (mixer_odd, ada_w1, ada_b1, mix_pre1, mix_post1, ffn_pre1, ffn_post1, ffn_gate1, ffn_up1, ffn_down1),
    )
    for i in range(DEPTH):
        mixer, aw, ab, m_pre, m_post, f_pre, f_post, wg, wu, wd = layers[i]
        sh_m, sc_m, gt_m, sh_f, sc_f, gt_f = ada_modulation(c, aw, ab)
        h = rmsnorm(x, m_pre) * (1.0 + sc_m) + sh_m
        x = x + gt_m * rmsnorm(mixer(h), m_post)
        h = rmsnorm(x, f_pre) * (1.0 + sc_f) + sh_f
        x = x + gt_f * rmsnorm(swiglu(h, wg, wu, wd), f_post)
    return x
```

```python
import numpy as np
import concourse.bass as bass
import concourse.mybir as mybir
from concourse.bass_utils import run_bass_kernel_spmd

F32 = mybir.dt.float32
BF16 = mybir.dt.bfloat16
ALU = mybir.AluOpType
AF = mybir.ActivationFunctionType
AX = mybir.AxisListType

NCORES = 8
D = 2048
L = 16384
EPS = 1e-6
FFN_H = 5632


class _Eng:
    def __init__(self, name, eng, sem):
        self.name, self.eng, self.sem = name, eng, sem
        self.cnt = 0
        self.seen = {}


class Buf:
    def __init__(self, kb, name, ap):
        self.kb, self.name, self.ap = kb, name, ap
        self.w = {}
        self.r = {}
        self.dsem = None
        self.dcnt = 0
        self.dw = []
        self.dr = []

    def __getitem__(self, k):
        return self.ap[k]


class KB:
    def __init__(self):
        self.nc = bass.Bass("TRN2", target_bir_lowering=False)
        nc = self.nc
        self.E = {}
        for n in ("tensor", "vector", "scalar", "gpsimd", "sync"):
            self.E[n] = _Eng(n, getattr(nc, n), nc.alloc_semaphore("es_" + n))
        self.bufs = []
        self.ps_list = []
        self.ps_i = 0
        self._uid = 0

    def sb(self, name, shape, dtype):
        t = self.nc.alloc_sbuf_tensor("sb_" + name, list(shape), dtype)
        b = Buf(self, name, t[:])
        self.bufs.append(b)
        return b

    def view(self, name, ap):
        b = Buf(self, name, ap)
        self.bufs.append(b)
        return b

    def alloc_psum(self, n=8):
        for i in range(n):
            t = self.nc.alloc_psum_tensor("ps%d" % i, [128, 512], F32)
            b = Buf(self, "ps%d" % i, t[:])
            self.bufs.append(b)
            self.ps_list.append(b)

    def next_ps(self):
        b = self.ps_list[self.ps_i % len(self.ps_list)]
        self.ps_i += 1
        return b

    def din(self, name, shape, dtype=F32):
        return self.nc.dram_tensor(name, list(shape), dtype, kind="ExternalInput").ap()

    def dout(self, name, shape, dtype=F32):
        return self.nc.dram_tensor(name, list(shape), dtype, kind="ExternalOutput").ap()

    def _needs(self, ename, R, W):
        need = []
        for b in R:
            for en, c in b.w.items():
                if not (en == ename and ename == "tensor"):
                    need.append((("e", en), self.E[en].sem, c))
            for sem, val in b.dw:
                need.append((("d", id(sem)), sem, val))
        for b in W:
            for en, c in b.w.items():
                if en != ename:
                    need.append((("e", en), self.E[en].sem, c))
            for en, c in b.r.items():
                if en != ename:
                    need.append((("e", en), self.E[en].sem, c))
            for sem, val in b.dw + b.dr:
                need.append((("d", id(sem)), sem, val))
        return need

    def _emit_waits(self, e, need):
        best = {}
        for key, sem, val in need:
            if key not in best or best[key][1] < val:
                best[key] = (sem, val)
        for key, (sem, val) in best.items():
            if e.seen.get(key, 0) < val:
                e.eng.wait_ge(sem, val)
                e.seen[key] = val

    def op(self, ename, fn, R=(), W=()):
        e = self.E[ename]
        self._emit_waits(e, self._needs(ename, R, W))
        ins = fn(e.eng)
        e.cnt += 1
        ins.then_inc(e.sem, 1)
        for b in R:
            b.r[ename] = e.cnt
        for b in W:
            b.w = {ename: e.cnt}
            b.r = {}
            b.dw = []
            b.dr = []
        return ins

    def I(self, ename, meth, R=(), W=(), **kw):
        return self.op(ename, lambda eng: getattr(eng, meth)(**kw), R, W)

    def dma(self, q, out, in_, sbuf, load, R=(), W=(), group=False):
        e = self.E[q]
        R = list(R)
        W = list(W)
        saved = None
        if load:
            if group:
                saved = sbuf.dw
                sbuf.dw = []
            W.append(sbuf)
        else:
            R.append(sbuf)
        self._emit_waits(e, self._needs("dma", R, W))
        if sbuf.dsem is None:
            sbuf.dsem = self.nc.alloc_semaphore("ds_%s_%d" % (sbuf.name, self._uid))
            self._uid += 1
        ins = e.eng.dma_start(out=out, in_=in_)
        sbuf.dcnt += 1
        ins.then_inc(sbuf.dsem, 16)
        ev = (sbuf.dsem, sbuf.dcnt * 16)
        for b in W:
            b.w = {}
            b.r = {}
            b.dw = [ev]
            b.dr = []
        for b in R:
            b.dr = [x for x in b.dr if x[0] is not ev[0]] + [ev]
        return ins

    def finish(self):
        e = self.E["sync"]
        for b in self.bufs:
            if b.dsem is not None and b.dcnt > 0:
                e.eng.wait_ge(b.dsem, b.dcnt * 16)
        for n in ("tensor", "vector", "scalar", "gpsimd"):
            en = self.E[n]
            if en.cnt:
                e.eng.wait_ge(en.sem, en.cnt)

    def mm(self, ps, out, lhsT, rhs, start, stop, R):
        return self.op("tensor", lambda eng: eng.matmul(out, lhsT, rhs, start=start, stop=stop), R, [ps])

    def act(self, out, in_, func, R, W, bias=None, scale=1.0, eng="scalar"):
        kw = dict(out=out, in_=in_, func=func, scale=scale)
        if bias is not None:
            kw["bias"] = bias
        return self.op("scalar", lambda e: e.activation(**kw), R, W)

    def tt(self, eng, out, in0, in1, op, R, W):
        return self.op(eng, lambda e: e.tensor_tensor(out=out, in0=in0, in1=in1, op=op), R, W)

    def stt(self, out, in0, scalar, in1, op0, op1, R, W):
        return self.op("vector", lambda e: e.scalar_tensor_tensor(out=out, in0=in0, scalar=scalar, in1=in1,
                                                                   op0=op0, op1=op1), R, W)

    def ts(self, eng, out, in0, s1, op0, R, W, s2=None, op1=None):
        if op1 is None:
            return self.op(eng, lambda e: e.tensor_scalar(out=out, in0=in0, scalar1=s1, scalar2=None, op0=op0), R, W)
        return self.op(eng, lambda e: e.tensor_scalar(out=out, in0=in0, scalar1=s1, scalar2=s2, op0=op0, op1=op1), R, W)

    def copy(self, eng, out, in_, R, W):
        if eng == "scalar":
            return self.op("scalar", lambda e: e.copy(out=out, in_=in_), R, W)
        return self.op(eng, lambda e: e.tensor_copy(out=out, in_=in_), R, W)

    def memset(self, eng, buf, ap, val):
        return self.op(eng, lambda e: e.memset(ap, val), [], [buf])


T = 1024
HT = 512


class Dense:
    def __init__(self, kb, n_wb=4, n_ost=4):
        self.kb = kb
        self.n_ost = n_ost
        kb.alloc_psum(8)
        self.ones = kb.sb("ones_bf", [128, 128], BF16)
        kb.memset("vector", self.ones, self.ones[:], 1.0)
        self.R1t = kb.sb("R1", [128, 16, T], F32)
        self.R1 = [kb.view("R1_%d" % i, self.R1t.ap[:, i, :]) for i in range(16)]
        self.R3t = kb.sb("R3", [128, 16, T], BF16)
        self.R3 = [kb.view("R3_%d" % i, self.R3t.ap[:, i, :]) for i in range(16)]
        self.wb = [kb.sb("wb%d" % i, [128, 4096], BF16) for i in range(n_wb)]
        self.wi = 0
        self.sq = [kb.sb("sq%d" % i, [128, T], BF16) for i in range(2)]
        self.tmp = [kb.sb("tmp%d" % i, [128, T], F32) for i in range(2)]
        self.rstd = kb.sb("rstd", [128, T], F32)
        self.ost = [kb.sb("ost%d" % i, [128, HT], F32) for i in range(n_ost)]
        self.oi = 0
        self.evi = 0

    def next_wb(self):
        b = self.wb[self.wi % len(self.wb)]
        self.wi += 1
        return b

    def load_w(self, Wd, k0, k1, c0, nb):
        kb = self.kb
        wb = self.next_wb()
        kc = k1 - k0
        view = wb.ap[:, :kc * nb].rearrange("p (k n) -> p k n", n=nb)
        src = Wd.rearrange("(k p) n -> p k n", p=128)[:, k0:k1, c0:c0 + nb]
        kb.dma("gpsimd", view, src, wb, load=True)
        return wb, view

    def sumsq_rstd(self, src, nfeat_chunks, scale, out_rstd, width=T):
        kb = self.kb
        nh = width // HT
        pss = [kb.next_ps() for _ in range(nh)]
        n = len(src)
        for kc in range(n):
            sq = self.sq[kc % 2]
            kb.act(sq[:, :width], src[kc][:, :width], AF.Square, [src[kc]], [sq])
            for h in range(nh):
                kb.mm(pss[h], pss[h][:, :HT], self.ones[:, :], sq[:, h * HT:(h + 1) * HT], kc == 0, kc == n - 1,
                      [self.ones, sq])
        for h in range(nh):
            kb.act(out_rstd[:, h * HT:(h + 1) * HT], pss[h][:, :HT], AF.Sqrt, [pss[h]], [out_rstd], bias=self.epsb[:, 0:1],
                   scale=scale)
        kb.op("vector", lambda e: e.reciprocal(out=out_rstd[:, :width], in_=out_rstd[:, :width]), [out_rstd], [out_rstd])

    def consts(self):
        kb = self.kb
        self.epsb = kb.sb("epsb", [128, 1], F32)
        kb.memset("vector", self.epsb, self.epsb[:], EPS)

    def prenorm(self, src, gs, sh, dst):
        kb = self.kb
        self.sumsq_rstd(src, 16, 1.0 / D, self.rstd)
        for kc in range(16):
            tmp = self.tmp[kc % 2]
            kb.stt(tmp[:, :], src[kc][:, :], gs[:, kc:kc + 1], self.rstd[:, :], ALU.mult, ALU.mult,
                   [src[kc], gs, self.rstd], [tmp])
            kb.act(dst[kc][:, :], tmp[:, :], AF.Identity, [tmp, sh], [dst[kc]], bias=sh[:, kc:kc + 1])

    def proj(self, hsrc, Wd, ncols, epi, kgroups=None, nb=256, c_base=0):
        kb = self.kb
        KC = len(hsrc)
        if kgroups is None:
            kgroups = [(0, KC)]
        c = 0
        while c < ncols:
            cb = min(nb, ncols - c)
            views = []
            for (k0, k1) in kgroups:
                wbuf, v = self.load_w(Wd, k0, k1, c_base + c, cb)
                views.append((wbuf, v, k0, k1))
            m = 0
            while m < cb:
                mc = min(128, cb - m)
                for half in range(T // HT):
                    ps = kb.next_ps()
                    first = True
                    for (wbuf, v, k0, k1) in views:
                        for k in range(k0, k1):
                            kb.mm(ps, ps[:mc, :HT], v[:, k - k0, m:m + mc], hsrc[k][:, half * HT:(half + 1) * HT],
                                  first, (k == KC - 1), [wbuf, hsrc[k]])
                            first = False
                    epi(c + m, mc, half, ps)
                m += 128
            c += cb

    def evac_engine(self):
        self.evi += 1
        return "scalar" if self.evi % 2 else "vector"

    def epi_store(self, out_dram, t0):
        kb = self

        def epi(col, mc, half, ps):
            st = self.ost[self.oi % self.n_ost]
            self.oi += 1
            self.kb.copy(self.evac_engine(), st[:mc, :], ps[:mc, :HT], [ps], [st])
            self.kb.dma("sync", out_dram[col:col + mc, t0 + half * HT:t0 + (half + 1) * HT], st[:mc, :], st, load=False)
        return epi


    def alloc_ffn(self):
        kb = self.kb
        self.Ht = kb.sb("H", [128, 22, T], BF16)
        self.H = [kb.view("H_%d" % i, self.Ht.ap[:, i, :]) for i in range(22)]
        self.xs = [kb.sb("xs%d" % i, [128, T], F32) for i in range(2)]

    def postnorm_residual(self, gw, res_src, res_bufs, dst, dst_bufs):
        kb = self.kb
        self.sumsq_rstd(self.R1, 16, 1.0 / D, self.rstd)
        for kc in range(16):
            xs = self.xs[kc % 2]
            kb.dma("sync", xs[:, :], res_src(kc), xs, load=True, R=[res_bufs[kc]] if res_bufs else [])
            tmp = self.tmp[kc % 2]
            r1 = self.R1[kc]
            kb.stt(tmp[:, :], r1[:, :], gw[:, kc:kc + 1], self.rstd[:, :], ALU.mult, ALU.mult, [r1, gw, self.rstd], [tmp])
            kb.tt("vector", r1[:, :], tmp[:, :], xs[:, :], ALU.add, [tmp, xs], [r1])
            kb.dma("sync", dst(kc), r1[:, :], r1, load=False, W=[dst_bufs[kc]] if dst_bufs else [])

    def ffn(self, Wg, Wu, Wd):
        kb = self.kb
        HH = FFN_H // 2
        wgv = Wg.rearrange("(k p) n -> p k n", p=128)
        wuv = Wu.rearrange("(k p) n -> p k n", p=128)
        for hh in range(2):
            for blk in range(22):
                col = hh * HH + blk * 128
                wb = self.next_wb()
                vg = wb.ap[:, 0:2048].rearrange("p (k n) -> p k n", n=128)
                vu = wb.ap[:, 2048:4096].rearrange("p (k n) -> p k n", n=128)
                kb.dma("gpsimd", vg, wgv[:, :, col:col + 128], wb, load=True)
                kb.dma("gpsimd", vu, wuv[:, :, col:col + 128], wb, load=True, group=True)
                for half in range(T // HT):
                    pg = kb.next_ps()
                    pu = kb.next_ps()
                    hs = slice(half * HT, (half + 1) * HT)
                    for k in range(16):
                        kb.mm(pg, pg[:, :HT], vg[:, k, :], self.R3[k][:, hs], k == 0, k == 15, [wb, self.R3[k]])
                    for k in range(16):
                        kb.mm(pu, pu[:, :HT], vu[:, k, :], self.R3[k][:, hs], k == 0, k == 15, [wb, self.R3[k]])
                    sg = self.tmp[half % 2]
                    kb.act(sg[:, :HT], pg[:, :HT], AF.Silu, [pg], [sg])
                    kb.tt("vector", self.H[blk][:, hs], sg[:, :HT], pu[:, :HT], ALU.mult, [sg, pu], [self.H[blk]])

            def epi(col, mc, half, ps, hh=hh):
                m = col // 128
                hs = slice(half * HT, (half + 1) * HT)
                r1 = self.R1[m]
                if hh == 0:
                    kb.copy(self.evac_engine(), r1[:, hs], ps[:, :HT], [ps], [r1])
                else:
                    kb.tt("vector", r1[:, hs], r1[:, hs], ps[:, :HT], ALU.add, [r1, ps], [r1])
            self.proj(self.H, Wd[hh * HH:(hh + 1) * HH, :], D, epi, nb=128)

    def epi_to_R1(self):
        def epi(col, mc, half, ps):
            m = col // 128
            r1 = self.R1[m]
            self.kb.copy(self.evac_engine(), r1[:, half * HT:(half + 1) * HT], ps[:, :HT], [ps], [r1])
        return epi

    def tail0(self, yS5T, ogT, zT, glu_w, glu_b, gnw, t0):
        kb = self.kb
        for kc in range(8):
            kb.dma("sync", self.R1[kc][:, :], yS5T[kc * 128:(kc + 1) * 128, t0:t0 + T], self.R1[kc], load=True)
            kb.copy("scalar" if kc % 2 else "vector", self.H[kc][:, :], self.R1[kc][:, :], [self.R1[kc]], [self.H[kc]])

        def epi_glu(col, mc, half, ps):
            m = col // 128
            hs = slice(half * HT, (half + 1) * HT)
            sg = self.tmp[half % 2]
            kb.act(sg[:, :HT], ps[:, :HT], AF.Sigmoid, [ps, glu_b], [sg], bias=glu_b[:, m:m + 1])
            kb.tt("vector", self.R3[m][:, hs], self.R1[m][:, hs], sg[:, :HT], ALU.mult, [self.R1[m], sg], [self.R3[m]])
        self.proj(self.H[0:8], glu_w, 1024, epi_glu)
        for hd in range(8):
            ob = self.R1[8 + hd]
            zb = self.R1[hd]
            kb.dma("sync", ob[:, :], ogT[hd * 128:(hd + 1) * 128, t0:t0 + T], ob, load=True)
            kb.dma("sync", zb[:, :], zT[hd * 128:(hd + 1) * 128, t0:t0 + T], zb, load=True)
            self.sumsq_rstd([ob], 1, 1.0 / 128, self.rstd)
            t1 = self.tmp[0]
            t2 = self.tmp[1]
            kb.tt("vector", t1[:, :], ob[:, :], self.rstd[:, :], ALU.mult, [ob, self.rstd], [t1])
            kb.act(t2[:, :], zb[:, :], AF.Silu, [zb], [t2])
            kb.stt(self.R3[8 + hd][:, :], t1[:, :], gnw[:, 0:1], t2[:, :], ALU.mult, ALU.mult, [t1, gnw, t2],
                   [self.R3[8 + hd]])

    def tail1(self, oT, rT, gnw4, t0):
        kb = self.kb
        for hd in range(4):
            cs = list(range(4 * hd, 4 * hd + 4))
            for c in cs:
                kb.dma("sync", self.R1[c][:, :], oT[c * 128:(c + 1) * 128, t0:t0 + T], self.R1[c], load=True)
            self.sumsq_rstd([self.R1[c] for c in cs], 4, 1.0 / 512, self.rstd)
            for j, c in enumerate(cs):
                xs = self.xs[j % 2]
                kb.dma("sync", xs[:, :], rT[c * 128:(c + 1) * 128, t0:t0 + T], xs, load=True)
                t1 = self.tmp[0]
                t2 = self.tmp[1]
                kb.act(t2[:, :], xs[:, :], AF.Silu, [xs], [t2])
                kb.stt(t1[:, :], self.R1[c][:, :], gnw4[:, j:j + 1], self.rstd[:, :], ALU.mult, ALU.mult,
                       [self.R1[c], gnw4, self.rstd], [t1])
                kb.tt("vector", self.R3[c][:, :], t1[:, :], t2[:, :], ALU.mult, [t1, t2], [self.R3[c]])

    def load_x(self, xT, t0):
        for kc in range(16):
            self.kb.dma("sync", self.R1[kc][:, :], xT[kc * 128:(kc + 1) * 128, t0:t0 + T], self.R1[kc], load=True)

    def mod_consts(self, modT_d, wpre_d, wpost_d, which, tag=""):
        kb = self.kb
        if not hasattr(self, "modT" + tag):
            setattr(self, "modT" + tag, kb.sb("modT" + tag, [128, 96], F32))
            m_ = getattr(self, "modT" + tag)
            kb.dma("sync", m_[:, :], modT_d[:, :], m_, load=True)
        self.modT = getattr(self, "modT" + tag)
        which = which + tag
        o = 0 if which[0] == "m" else 48
        res = {}
        if wpre_d is not None:
            wpre = kb.sb("wpre_" + which, [128, 16], F32)
            kb.dma("sync", wpre[:, :], wpre_d[:, :], wpre, load=True)
            gs = kb.sb("gs_" + which, [128, 16], F32)
            kb.stt(gs[:, :], self.modT[:, o + 16:o + 32], 1.0, wpre[:, :], ALU.add, ALU.mult, [self.modT, wpre], [gs])
            sh = kb.sb("sh_" + which, [128, 16], F32)
            kb.copy("vector", sh[:, :], self.modT[:, o:o + 16], [self.modT], [sh])
            res["gs"] = gs
            res["sh"] = sh
        if wpost_d is not None:
            wpost = kb.sb("wpost_" + which, [128, 16], F32)
            kb.dma("sync", wpost[:, :], wpost_d[:, :], wpost, load=True)
            gw = kb.sb("gw_" + which, [128, 16], F32)
            kb.tt("vector", gw[:, :], self.modT[:, o + 32:o + 48], wpost[:, :], ALU.mult, [self.modT, wpost], [gw])
            res["gw"] = gw
        return res


def build_L0():
    kb = KB()
    kb.alloc_psum(8)
    NJ = 24
    c_d = kb.din("c", [128, 16])
    w_d = kb.din("ada_w", [2048, NJ * 128])
    b_d = kb.din("ada_b", [128, NJ])
    out_d = kb.dout("mod", [128, NJ])
    cs = kb.sb("c_sb", [128, 16], F32)
    sc = kb.sb("silu_c", [128, 16], F32)
    bs = kb.sb("b_sb", [128, NJ], F32)
    os_ = kb.sb("o_sb", [128, NJ], F32)
    wf = [kb.sb("wf%d" % i, [128, 16, 128], F32) for i in range(3)]
    kb.dma("sync", cs[:, :], c_d[:, :], cs, load=True)
    kb.dma("sync", bs[:, :], b_d[:, :], bs, load=True)
    kb.act(sc[:, :], cs[:, :], AF.Silu, [cs], [sc])
    wv = w_d.rearrange("(k p) n -> p k n", p=128)
    for j in range(NJ):
        w = wf[j % 3]
        kb.dma("sync" if j % 2 == 0 else "gpsimd", w[:, :, :], wv[:, :, j * 128:(j + 1) * 128], w, load=True)
        ps = kb.next_ps()
        for kc in range(16):
            kb.mm(ps, ps[:, 0:1], w[:, kc, :], sc[:, kc:kc + 1], kc == 0, kc == 15, [w, sc])
        kb.tt("vector", os_[:, j:j + 1], ps[:, 0:1], bs[:, j:j + 1], ALU.add, [ps, bs], [os_])
    kb.dma("sync", out_d[:, :], os_[:, :], os_, load=False)
    kb.finish()
    return kb.nc


def build_LA(ntok, ncols):
    kb = KB()
    dn = Dense(kb)
    dn.consts()
    xT = kb.din("xT", [2048, ntok])
    modT = kb.din("modT", [128, 96])
    wpre = kb.din("wpre", [128, 16])
    W = kb.din("w_in", [2048, ncols])
    PT = kb.dout("PT", [ncols, ntok])
    mc = dn.mod_consts(modT, wpre, None, "m")
    for t0 in range(0, ntok, T):
        dn.load_x(xT, t0)
        dn.prenorm(dn.R1, mc["gs"], mc["sh"], dn.R3)
        dn.proj(dn.R3, W, ncols, dn.epi_store(PT, t0))
    kb.finish()
    return kb.nc


def build_LC(ntok, layer, ncols_next=0):
    kb = KB()
    dn = Dense(kb, n_wb=3, n_ost=2)
    dn.consts()
    dn.alloc_ffn()
    nc = kb.nc
    xT = kb.din("xT", [2048, ntok])
    modT = kb.din("modT", [128, 96])
    wpost_m = kb.din("wpost_m", [128, 16])
    wpre_f = kb.din("wpre_f", [128, 16])
    wpost_f = kb.din("wpost_f", [128, 16])
    w_out = kb.din("w_out", [2048, 2048])
    Wg = kb.din("ffn_gate", [2048, FFN_H])
    Wu = kb.din("ffn_up", [2048, FFN_H])
    Wd = kb.din("ffn_down", [FFN_H, 2048])
    if layer == 0:
        yS5T = kb.din("yS5T", [1024, ntok])
        ogT = kb.din("ogT", [1024, ntok])
        zT = kb.din("zT", [1024, ntok])
        glu_w = kb.din("glu_w", [1024, 1024])
        glu_b_d = kb.din("glu_b", [128, 8])
        gnw_d = kb.din("gnw", [128, 1])
        glu_b = kb.sb("glu_b", [128, 8], F32)
        gnw = kb.sb("gnw", [128, 1], F32)
        kb.dma("sync", glu_b[:, :], glu_b_d[:, :], glu_b, load=True)
        kb.dma("sync", gnw[:, :], gnw_d[:, :], gnw, load=True)
    else:
        oT = kb.din("oT", [2048, ntok])
        rT = kb.din("rT", [2048, ntok])
        gnw_d = kb.din("gnw4", [128, 4])
        gnw4 = kb.sb("gnw4", [128, 4], F32)
        kb.dma("sync", gnw4[:, :], gnw_d[:, :], gnw4, load=True)
    xoutT = kb.dout("xoutT", [2048, ntok])
    xmidT = kb.dout("xmidT", [2048, ntok])
    xmid_b = [kb.view("xmid_%d" % i, xmidT[i * 128:(i + 1) * 128, :]) for i in range(16)]
    mcm = dn.mod_consts(modT, None, wpost_m, "m")
    mcf = dn.mod_consts(modT, wpre_f, wpost_f, "f")
    if ncols_next:
        modTn = kb.din("modT_n", [128, 96])
        wpre_n = kb.din("wpre_n", [128, 16])
        w_in_n = kb.din("w_in_n", [2048, ncols_next])
        PTn = kb.dout("PTn", [ncols_next, ntok])
        mcn = dn.mod_consts(modTn, wpre_n, None, "m", tag="n")
    for t0 in range(0, ntok, T):
        if layer == 0:
            dn.tail0(yS5T, ogT, zT, glu_w, glu_b, gnw, t0)
        else:
            dn.tail1(oT, rT, gnw4, t0)
        dn.proj(dn.R3, w_out, D, dn.epi_to_R1())
        dn.postnorm_residual(mcm["gw"], lambda kc: xT[kc * 128:(kc + 1) * 128, t0:t0 + T], None,
                             lambda kc: xmidT[kc * 128:(kc + 1) * 128, t0:t0 + T], xmid_b)
        dn.prenorm(dn.R1, mcf["gs"], mcf["sh"], dn.R3)
        dn.ffn(Wg, Wu, Wd)
        dn.postnorm_residual(mcf["gw"], lambda kc: xmidT[kc * 128:(kc + 1) * 128, t0:t0 + T], xmid_b,
                             lambda kc: xoutT[kc * 128:(kc + 1) * 128, t0:t0 + T], None)
        if ncols_next:
            dn.prenorm(dn.R1, mcn["gs"], mcn["sh"], dn.R3)
            dn.proj(dn.R3, w_in_n, ncols_next, dn.epi_store(PTn, t0))
    kb.finish()
    return kb.nc


SC = 512
CH = 64


def role_psum(kb, names):
    d = {}
    for n in names:
        t = kb.nc.alloc_psum_tensor("ps_" + n, [128, 512], F32)
        b = Buf(kb, "ps_" + n, t[:])
        kb.bufs.append(b)
        d[n] = b
    return d


class Rot:
    def __init__(self, items):
        self.items, self.i = items, 0

    def next(self):
        b = self.items[self.i % len(self.items)]
        self.i += 1
        return b


def build_LD(Ltot):
    kb = KB()
    P = role_psum(kb, ["o0", "o1", "S0", "S1", "a", "m0", "m1", "m2"])
    misc = Rot([P["m0"], P["m1"], P["m2"]])
    qT = kb.din("qT", [256, Ltot])
    kT = kb.din("kT", [256, Ltot])
    vT = kb.din("vT", [256, Ltot])
    glT = kb.din("glT", [16, Ltot])
    w2_d = kb.din("w2", [16, 256])
    ngb_d = kb.din("gb", [128, 2])
    ident_d = kb.din("ident", [128, 128])
    maskU_d = kb.din("maskU", [64, 512])
    rmask_d = kb.din("rmask", [128, 512])
    oT = kb.dout("oT", [256, Ltot])

    def const(name, shape, src):
        b = kb.sb(name, shape, F32)
        kb.dma("sync", b[:], src, b, load=True)
        return b
    w2 = const("w2", [16, 256], w2_d[:, :])
    gb = const("gb", [128, 2], ngb_d[:, :])
    ident = const("ident", [128, 128], ident_d[:, :])
    maskU = const("maskU", [64, 512], maskU_d[:, :])
    rmask = const("rmask", [128, 512], rmask_d[:, :])
    ngb = kb.sb("ngb", [128, 2], F32)
    kb.ts("vector", ngb[:, :], gb[:, :], -1.0, ALU.mult, [gb], [ngb])

    def dbl(name, shape, dtype, n=2):
        return Rot([kb.sb("%s%d" % (name, i), shape, dtype) for i in range(n)])
    qf = [dbl("qf%d" % kt, [128, SC], F32) for kt in range(2)]
    kf = [dbl("kf%d" % kt, [128, SC], F32) for kt in range(2)]
    vf = [dbl("vf%d" % kt, [128, SC], F32) for kt in range(2)]
    gl = dbl("gl", [16, SC], F32)
    e1 = [kb.sb("e1_%d" % kt, [128, SC], F32) for kt in range(2)]
    cum = [kb.sb("cum%d" % kt, [128, SC], F32) for kt in range(2)]
    eq = [dbl("eq%d" % kt, [128, SC], F32) for kt in range(2)]
    ek = [kb.sb("ek%d" % kt, [128, SC], F32) for kt in range(2)]
    dd = [kb.sb("dd%d" % kt, [128, SC], F32) for kt in range(2)]
    qt = [dbl("qt%d" % kt, [128, SC], BF16) for kt in range(2)]
    ktl = [dbl("kt%d" % kt, [128, SC], BF16) for kt in range(2)]
    kdT = [kb.sb("kdT%d" % kt, [128, SC], F32) for kt in range(2)]
    ktok = dbl("ktok", [64, 8, 256], BF16)
    vtok = dbl("vtok", [64, 8, 256], BF16)
    attnT = dbl("attnT", [64, SC], BF16)
    S = [kb.sb("S%d" % kt, [128, 256], F32) for kt in range(2)]
    Sb = [kb.sb("Sb%d" % kt, [128, 256], BF16) for kt in range(2)]
    ost = dbl("ost", [128, SC], F32, 4)
    for kt in range(2):
        kb.memset("vector", S[kt], S[kt][:, :], 0.0)
        kb.memset("vector", Sb[kt], Sb[kt][:, :], 0.0)
    scale_q = 256 ** -0.5
    nsc = Ltot // SC
    for s in range(nsc):
        ts_ = slice(s * SC, (s + 1) * SC)
        q_ = [qf[kt].next() for kt in range(2)]
        k_ = [kf[kt].next() for kt in range(2)]
        v_ = [vf[kt].next() for kt in range(2)]
        g_ = gl.next()
        kb.dma("sync", g_[:, :], glT[:, ts_], g_, load=True)
        for kt in range(2):
            kb.dma("sync", q_[kt][:, :], qT[kt * 128:(kt + 1) * 128, ts_], q_[kt], load=True)
            kb.dma("sync", k_[kt][:, :], kT[kt * 128:(kt + 1) * 128, ts_], k_[kt], load=True)
            kb.dma("gpsimd", v_[kt][:, :], vT[kt * 128:(kt + 1) * 128, ts_], v_[kt], load=True)
        eq_ = [eq[kt].next() for kt in range(2)]
        qt_ = [qt[kt].next() for kt in range(2)]
        kt_ = [ktl[kt].next() for kt in range(2)]
        for kt in range(2):
            pg = misc.next()
            kb.mm(pg, pg[:, :SC], w2[:, kt * 128:(kt + 1) * 128], g_[:, :], True, True, [w2, g_])
            kb.act(e1[kt][:, :], pg[:, :SC], AF.Exp, [pg, ngb], [e1[kt]], bias=ngb[:, kt:kt + 1], scale=-1.0)
            kb.act(e1[kt][:, :], e1[kt][:, :], AF.Ln, [e1[kt]], [e1[kt]], bias=1.0)
            kb.op("vector", lambda e, kt=kt: e.tensor_tensor_scan(out=cum[kt][:, :], data0=rmask[:, :], data1=e1[kt][:, :],
                                                                  initial=0.0, op0=ALU.mult, op1=ALU.add),
                  [rmask, e1[kt]], [cum[kt]])
            kb.act(eq_[kt][:, :], cum[kt][:, :], AF.Exp, [cum[kt]], [eq_[kt]], scale=-1.0 / 16)
            kb.act(ek[kt][:, :], cum[kt][:, :], AF.Exp, [cum[kt]], [ek[kt]], scale=1.0 / 16)
            c3 = cum[kt].ap.rearrange("p (c t) -> p c t", t=CH)
            d3 = dd[kt].ap.rearrange("p (c t) -> p c t", t=CH)
            kb.tt("vector", d3, c3, c3[:, :, CH - 1:CH].to_broadcast([128, SC // CH, CH]), ALU.subtract, [cum[kt]], [dd[kt]])
            kb.act(dd[kt][:, :], dd[kt][:, :], AF.Exp, [dd[kt]], [dd[kt]], scale=1.0 / 16)
            kb.stt(qt_[kt][:, :], q_[kt][:, :], scale_q, eq_[kt][:, :], ALU.mult, ALU.mult, [q_[kt], eq_[kt]], [qt_[kt]])
            kb.tt("vector", kt_[kt][:, :], k_[kt][:, :], ek[kt][:, :], ALU.mult, [k_[kt], ek[kt]], [kt_[kt]])
            kb.tt("gpsimd", kdT[kt][:, :], k_[kt][:, :], dd[kt][:, :], ALU.mult, [k_[kt], dd[kt]], [kdT[kt]])
        ktok_ = ktok.next()
        vtok_ = vtok.next()
        for (srcs, dst) in ((kdT, ktok_), (v_, vtok_)):
            for c2 in range(4):
                pt = misc.next()
                for cc in range(2):
                    n = c2 * 2 + cc
                    for kt in range(2):
                        kb.op("tensor", lambda e, pt=pt, cc=cc, kt=kt, n=n, srcs=srcs: e.transpose(
                            out=pt[:64, cc * 256 + kt * 128: cc * 256 + (kt + 1) * 128],
                            in_=srcs[kt][:, n * CH:(n + 1) * CH], identity=ident[:, :]), [srcs[kt], ident], [pt])
                kb.copy("scalar" if c2 % 2 else "vector", dst.ap[:, c2 * 2:c2 * 2 + 2, :],
                        pt[:64, :].rearrange("p (c d) -> p c d", d=256), [pt], [dst])
        pa = P["a"]
        for n in range(8):
            cs = slice(n * CH, (n + 1) * CH)
            for kt in range(2):
                kb.mm(pa, pa[:64, cs], kt_[kt][:, cs], qt_[kt][:, cs], kt == 0, kt == 1, [kt_[kt], qt_[kt]])
        at_ = attnT.next()
        kb.tt("vector", at_[:, :], pa[:64, :SC], maskU[:, :], ALU.mult, [pa, maskU], [at_])
        po = [P["o0"], P["o1"]]
        for n in range(8):
            cs = slice(n * CH, (n + 1) * CH)
            for dvt in range(2):
                ds_ = slice(dvt * 128, (dvt + 1) * 128)
                kb.mm(po[dvt], po[dvt][:, cs], vtok_[:, n, ds_], at_[:, cs], True, False, [vtok_, at_])
                for kt in range(2):
                    kb.mm(po[dvt], po[dvt][:, cs], Sb[kt][:, ds_], qt_[kt][:, cs], False, kt == 1, [Sb[kt], qt_[kt]])
            for kt in range(2):
                pS = P["S%d" % kt]
                kb.mm(pS, pS[:, :256], ktok_[:, n, kt * 128:(kt + 1) * 128], vtok_[:, n, :], True, True, [ktok_, vtok_])
                gl_col = eq_[kt][:, n * CH + CH - 1:n * CH + CH]
                kb.stt(Sb[kt][:, :], S[kt][:, :], gl_col, pS[:, :256], ALU.mult, ALU.add, [S[kt], eq_[kt], pS], [Sb[kt]])
                kb.stt(S[kt][:, :], S[kt][:, :], gl_col, pS[:, :256], ALU.mult, ALU.add, [S[kt], eq_[kt], pS], [S[kt]])
        for dvt in range(2):
            o_ = ost.next()
            kb.copy("scalar" if dvt else "vector", o_[:, :], po[dvt][:, :SC], [po[dvt]], [o_])
            kb.dma("sync", oT[dvt * 128:(dvt + 1) * 128, ts_], o_[:, :], o_, load=False)
    kb.finish()
    return kb.nc


def gla_consts():
    ident = np.eye(128, dtype=np.float32)
    U = (np.arange(64)[:, None] <= np.arange(64)[None, :]).astype(np.float32)
    maskU = np.tile(U, (1, 8))
    rmask = np.ones((128, 512), np.float32)
    rmask[:, ::64] = 0.0
    return {"ident": ident, "maskU": maskU, "rmask": rmask}


I32 = mybir.dt.int32
TWO_PI = 6.283185
GELU_C = 0.7978845608028654


def sincos_turns(kb, fin, W, name, tmps, P_=128):
    if not tmps:
        tmps["ki"] = kb.sb("sc_ki", [P_, W], I32)
        tmps["kf"] = kb.sb("sc_kf", [P_, W], F32)
        tmps["f"] = kb.sb("sc_f", [P_, W], F32)
        tmps["g"] = kb.sb("sc_g", [P_, W], F32)
    ki, kf, f, g = tmps["ki"], tmps["kf"], tmps["f"], tmps["g"]
    sn = kb.sb(name + "_sin", [P_, W], F32)
    cs = kb.sb(name + "_cos", [P_, W], F32)
    kb.copy("vector", ki[:, :], fin[:, :], [fin], [ki])
    kb.copy("vector", kf[:, :], ki[:, :], [ki], [kf])
    kb.tt("vector", f[:, :], fin[:, :], kf[:, :], ALU.subtract, [fin, kf], [f])

    def wrap(x):
        kb.ts("vector", g[:, :], x[:, :], 0.5, ALU.is_gt, [x], [g])
        kb.tt("vector", x[:, :], x[:, :], g[:, :], ALU.subtract, [x, g], [x])
        kb.ts("vector", g[:, :], x[:, :], -0.5, ALU.is_lt, [x], [g])
        kb.tt("vector", x[:, :], x[:, :], g[:, :], ALU.add, [x, g], [x])
    wrap(f)
    kb.act(sn[:, :], f[:, :], AF.Sin, [f], [sn], scale=TWO_PI)
    kb.ts("vector", f[:, :], f[:, :], 0.25, ALU.add, [f], [f])
    wrap(f)
    kb.act(cs[:, :], f[:, :], AF.Sin, [f], [cs], scale=TWO_PI)
    return sn, cs


def build_LS5(Ltot):
    kb = KB()
    P = role_psum(kb, ["re0", "im0", "re1", "im1", "y0", "y1", "x0", "x1"])
    uT = kb.din("uT", [128, Ltot])
    yT = kb.dout("yT", [128, Ltot])

    def const(name, shape):
        d_ = kb.din(name, shape)
        b = kb.sb(name, shape, F32)
        kb.dma("sync", b[:], d_[:], b, load=True)
        return b
    lamre_c = const("lamre_c", [128, 4])
    lamim_c = const("lamim_c", [128, 4])
    lstep_c = const("lstep_c", [128, 4])
    lamre_r = const("lamre_r", [128, 512])
    lamim_r = const("lamim_r", [128, 512])
    lstep_r = const("lstep_r", [128, 512])
    Bre = const("BblkT_re", [128, 512])
    Bim = const("BblkT_im", [128, 512])
    Cre = const("CblkT_re", [128, 4, 128])
    Cim = const("CblkT_im", [128, 4, 128])
    dcol = const("dcol", [128, 1])
    tau1 = const("tau1", [128, 512])
    V = "vector"
    W = 512

    def new(name, shape=None):
        return kb.sb(name, shape or [128, W], F32)
    dt_c = new("dt_c", [128, 4])
    rho_c = new("rho_c", [128, 4])
    th_c = new("th_c", [128, 4])
    kb.act(dt_c[:, :], lstep_c[:, :], AF.Exp, [lstep_c], [dt_c])
    kb.tt(V, rho_c[:, :], lamre_c[:, :], dt_c[:, :], ALU.mult, [lamre_c, dt_c], [rho_c])
    kb.act(rho_c[:, :], rho_c[:, :], AF.Exp, [rho_c], [rho_c])
    kb.stt(th_c[:, :], lamim_c[:, :], 1.0 / (2 * np.pi), dt_c[:, :], ALU.mult, ALU.mult, [lamim_c, dt_c], [th_c])
    thk = kb.sb("thk", [128, 4], I32)
    thf = new("thf", [128, 4])
    kb.copy(V, thk[:, :], th_c[:, :], [th_c], [thk])
    kb.copy(V, thf[:, :], thk[:, :], [thk], [thf])
    kb.tt(V, th_c[:, :], th_c[:, :], thf[:, :], ALU.subtract, [th_c, thf], [th_c])
    cosT, sinT, rhoT = [], [], []
    sct = {}
    ang = new("ang")
    for st in range(4):
        kb.ts(V, ang[:, :], tau1[:, :], th_c[:, st:st + 1], ALU.mult, [tau1, th_c], [ang])
        sn, cs = sincos_turns(kb, ang, W, "tab%d" % st, sct)
        sinT.append(sn)
        cosT.append(cs)
        rt = new("rhoT%d" % st)
        kb.ts(V, rt[:, :], tau1[:, :], 0.0, ALU.mult, [tau1], [rt], s2=rho_c[:, st:st + 1], op1=ALU.add)
        rhoT.append(rt)
    dt_r = new("dt_r")
    mag = new("mag")
    thr = new("thr")
    kb.act(dt_r[:, :], lstep_r[:, :], AF.Exp, [lstep_r], [dt_r])
    kb.tt(V, mag[:, :], lamre_r[:, :], dt_r[:, :], ALU.mult, [lamre_r, dt_r], [mag])
    kb.act(mag[:, :], mag[:, :], AF.Exp, [mag], [mag])
    kb.stt(thr[:, :], lamim_r[:, :], 1.0 / (2 * np.pi), dt_r[:, :], ALU.mult, ALU.mult, [lamim_r, dt_r], [thr])
    sn_r, cs_r = sincos_turns(kb, thr, W, "row", sct)
    nr = new("nr")
    ni = new("ni")
    kb.tt(V, nr[:, :], mag[:, :], cs_r[:, :], ALU.mult, [mag, cs_r], [nr])
    kb.ts(V, nr[:, :], nr[:, :], -1.0, ALU.add, [nr], [nr])
    kb.tt(V, ni[:, :], mag[:, :], sn_r[:, :], ALU.mult, [mag, sn_r], [ni])
    den = new("den")
    t_a = new("t_a")
    t_b = new("t_b")
    kb.tt(V, den[:, :], lamre_r[:, :], lamre_r[:, :], ALU.mult, [lamre_r], [den])
    kb.tt(V, t_a[:, :], lamim_r[:, :], lamim_r[:, :], ALU.mult, [lamim_r], [t_a])
    kb.tt(V, den[:, :], den[:, :], t_a[:, :], ALU.add, [den, t_a], [den])
    kb.op(V, lambda e: e.reciprocal(out=den[:, :], in_=den[:, :]), [den], [den])
    fre = new("fre")
    fim = new("fim")
    kb.tt(V, t_a[:, :], nr[:, :], lamre_r[:, :], ALU.mult, [nr, lamre_r], [t_a])
    kb.tt(V, t_b[:, :], ni[:, :], lamim_r[:, :], ALU.mult, [ni, lamim_r], [t_b])
    kb.tt(V, fre[:, :], t_a[:, :], t_b[:, :], ALU.add, [t_a, t_b], [fre])
    kb.tt(V, fre[:, :], fre[:, :], den[:, :], ALU.mult, [fre, den], [fre])
    kb.tt(V, t_a[:, :], ni[:, :], lamre_r[:, :], ALU.mult, [ni, lamre_r], [t_a])
    kb.tt(V, t_b[:, :], nr[:, :], lamim_r[:, :], ALU.mult, [nr, lamim_r], [t_b])
    kb.tt(V, fim[:, :], t_a[:, :], t_b[:, :], ALU.subtract, [t_a, t_b], [fim])
    kb.tt(V, fim[:, :], fim[:, :], den[:, :], ALU.mult, [fim, den], [fim])
    BbT_re = new("BbT_re")
    BbT_im = new("BbT_im")
    kb.tt(V, t_a[:, :], fre[:, :], Bre[:, :], ALU.mult, [fre, Bre], [t_a])
    kb.tt(V, t_b[:, :], fim[:, :], Bim[:, :], ALU.mult, [fim, Bim], [t_b])
    kb.tt(V, BbT_re[:, :], t_a[:, :], t_b[:, :], ALU.subtract, [t_a, t_b], [BbT_re])
    kb.tt(V, t_a[:, :], fre[:, :], Bim[:, :], ALU.mult, [fre, Bim], [t_a])
    kb.tt(V, t_b[:, :], fim[:, :], Bre[:, :], ALU.mult, [fim, Bre], [t_b])
    kb.tt(V, BbT_im[:, :], t_a[:, :], t_b[:, :], ALU.add, [t_a, t_b], [BbT_im])
    nCim = kb.sb("nCim", [128, 4, 128], F32)
    kb.ts(V, nCim[:, :, :], Cim[:, :, :], -1.0, ALU.mult, [Cim], [nCim])

    def dbl(name, n=2):
        return Rot([new("%s%d" % (name, i)) for i in range(n)])
    u_r = dbl("u_", 3)
    t1r, t2r, t3r, t4r = dbl("t1"), dbl("t2"), dbl("t3"), dbl("t4")
    brer, bimr = dbl("bre"), dbl("bim")
    zrer, zimr = dbl("zre"), dbl("zim")
    m1r, m2r, m3r, m4r = dbl("m1", 1), dbl("m2", 1), dbl("m3", 1), dbl("m4", 1)
    xre_r = [dbl("xre%d" % st) for st in range(4)]
    xim_r = [dbl("xim%d" % st) for st in range(4)]
    ypre = dbl("ypre")
    x2 = dbl("x2")
    ost = dbl("ost", 3)
    prev = [None] * 4
    psre = Rot([P["re0"], P["re1"]])
    psim = Rot([P["im0"], P["im1"]])
    psy = Rot([P["y0"], P["y1"]])
    G = "gpsimd"
    for s in range(Ltot // W):
        ts_ = slice(s * W, (s + 1) * W)
        u_ = u_r.next()
        kb.dma("sync", u_[:, :], uT[:, ts_], u_, load=True)
        py = psy.next()
        for st in range(4):
            pre, pim = psre.next(), psim.next()
            ss_ = slice(st * 128, (st + 1) * 128)
            kb.mm(pre, pre[:, :W], BbT_re[:, ss_], u_[:, :], True, True, [BbT_re, u_])
            kb.mm(pim, pim[:, :W], BbT_im[:, ss_], u_[:, :], True, True, [BbT_im, u_])
            t1, t2, t3, t4 = t1r.next(), t2r.next(), t3r.next(), t4r.next()
            bre, bim = brer.next(), bimr.next()
            kb.tt(V, t1[:, :], cosT[st][:, :], pre[:, :W], ALU.mult, [cosT[st], pre], [t1])
            kb.tt(V, t2[:, :], sinT[st][:, :], pim[:, :W], ALU.mult, [sinT[st], pim], [t2])
            kb.tt(G, bre[:, :], t1[:, :], t2[:, :], ALU.add, [t1, t2], [bre])
            kb.tt(V, t3[:, :], cosT[st][:, :], pim[:, :W], ALU.mult, [cosT[st], pim], [t3])
            kb.tt(V, t4[:, :], sinT[st][:, :], pre[:, :W], ALU.mult, [sinT[st], pre], [t4])
            kb.tt(G, bim[:, :], t3[:, :], t4[:, :], ALU.subtract, [t3, t4], [bim])
            zre, zim = zrer.next(), zimr.next()
            xre, xim = xre_r[st].next(), xim_r[st].next()
            if prev[st] is None:
                ire, iim, rdeps = 0.0, 0.0, []
            else:
                ire, iim = prev[st][0][:, W - 1:W], prev[st][1][:, W - 1:W]
                rdeps = [prev[st][0], prev[st][1]]
            kb.op(V, lambda e, zre=zre, bre=bre, ire=ire, st=st: e.tensor_tensor_scan(
                out=zre[:, :], data0=rhoT[st][:, :], data1=bre[:, :], initial=ire, op0=ALU.mult, op1=ALU.add),
                [rhoT[st], bre] + rdeps, [zre])
            kb.op(V, lambda e, zim=zim, bim=bim, iim=iim, st=st: e.tensor_tensor_scan(
                out=zim[:, :], data0=rhoT[st][:, :], data1=bim[:, :], initial=iim, op0=ALU.mult, op1=ALU.add),
                [rhoT[st], bim] + rdeps, [zim])
            m1, m2, m3, m4 = m1r.next(), m2r.next(), m3r.next(), m4r.next()
            kb.tt(G, m1[:, :], cosT[st][:, :], zre[:, :], ALU.mult, [cosT[st], zre], [m1])
            kb.tt(G, m2[:, :], sinT[st][:, :], zim[:, :], ALU.mult, [sinT[st], zim], [m2])
            kb.tt(V, xre[:, :], m1[:, :], m2[:, :], ALU.subtract, [m1, m2], [xre])
            kb.tt(G, m3[:, :], sinT[st][:, :], zre[:, :], ALU.mult, [sinT[st], zre], [m3])
            kb.tt(G, m4[:, :], cosT[st][:, :], zim[:, :], ALU.mult, [cosT[st], zim], [m4])
            kb.tt(V, xim[:, :], m3[:, :], m4[:, :], ALU.add, [m3, m4], [xim])
            prev[st] = (xre, xim)
            kb.mm(py, py[:, :W], Cre[:, st, :], xre[:, :], st == 0, False, [Cre, xre])
            kb.mm(py, py[:, :W], nCim[:, st, :], xim[:, :], False, st == 3, [nCim, xim])
        yp = ypre.next()
        kb.stt(yp[:, :], u_[:, :], dcol[:, 0:1], py[:, :W], ALU.mult, ALU.add, [u_, dcol, py], [yp])
        a_ = x2.next()
        kb.act(a_[:, :], yp[:, :], AF.Square, [yp], [a_])
        kb.ts(V, a_[:, :], a_[:, :], 0.044715, ALU.mult, [a_], [a_], s2=1.0, op1=ALU.add)
        kb.tt(G, a_[:, :], a_[:, :], yp[:, :], ALU.mult, [a_, yp], [a_])
        kb.act(a_[:, :], a_[:, :], AF.Sigmoid, [a_], [a_], scale=2.0 * GELU_C)
        o_ = ost.next()
        kb.tt(G, o_[:, :], a_[:, :], yp[:, :], ALU.mult, [a_, yp], [o_])
        kb.dma("sync", yT[:, ts_], o_[:, :], o_, load=False)
    kb.finish()
    return kb.nc


def s5_host_params(inp, core):
    g0 = core * 8
    lam_re = np.asarray(inp["s5_lambda_re"], np.float32)[g0:g0 + 8]
    lam_im = np.asarray(inp["s5_lambda_im"], np.float32)[g0:g0 + 8]
    lstep = np.asarray(inp["s5_log_step"], np.float32)[g0:g0 + 8]
    b_re = np.asarray(inp["s5_b_re"], np.float32)[g0:g0 + 8]
    b_im = np.asarray(inp["s5_b_im"], np.float32)[g0:g0 + 8]
    c_re = np.asarray(inp["s5_c_re"], np.float32)[g0:g0 + 8]
    c_im = np.asarray(inp["s5_c_im"], np.float32)[g0:g0 + 8]
    d = np.asarray(inp["s5_d"], np.float32)[g0:g0 + 8]
    row = lambda a: np.ascontiguousarray(np.broadcast_to(a.reshape(1, 512), (128, 512)))
    col = lambda a: np.ascontiguousarray(a.reshape(4, 128).T)
    ls_full = np.repeat(lstep[:, None], 64, axis=1)
    out = {"lamre_c": col(lam_re), "lamim_c": col(lam_im), "lstep_c": col(ls_full),
           "lamre_r": row(lam_re), "lamim_r": row(lam_im), "lstep_r": row(ls_full)}
    Bre = np.zeros((128, 512), np.float32)
    Bim = np.zeros((128, 512), np.float32)
    Cre = np.zeros((128, 4, 128), np.float32)
    Cim = np.zeros((128, 4, 128), np.float32)
    for g in range(8):
        Bre[g * 16:(g + 1) * 16, g * 64:(g + 1) * 64] = b_re[g].T
        Bim[g * 16:(g + 1) * 16, g * 64:(g + 1) * 64] = b_im[g].T
        st, h = g // 2, g % 2
        Cre[h * 64:(h + 1) * 64, st, g * 16:(g + 1) * 16] = c_re[g].T
        Cim[h * 64:(h + 1) * 64, st, g * 16:(g + 1) * 16] = c_im[g].T
    out.update({"BblkT_re": Bre, "BblkT_im": Bim, "CblkT_re": Cre, "CblkT_im": Cim,
                "dcol": np.ascontiguousarray(d.reshape(128, 1)),
                "tau1": np.ascontiguousarray(np.broadcast_to(np.arange(1, 513, dtype=np.float32)[None, :], (128, 512)))})
    return out


NEG_BIG = -30000.0


def build_LG(Ltot, debug=False, nst=6):
    kb = KB()
    P = role_psum(kb, ["o0", "o1", "v", "S", "m0", "m1", "m2", "m3"])
    misc = Rot([P["m0"], P["m1"], P["m2"], P["m3"]])
    po_r = Rot([P["o0"], P["o1"]])
    V, G = "vector", "gpsimd"
    W = SC
    NCH = W // CH
    qrT = kb.din("qrT", [128, Ltot])
    krT = kb.din("krT", [128, Ltot])
    vrT = kb.din("vrT", [128, Ltot])
    a_d = kb.din("a_row", [1, Ltot])
    b_d = kb.din("b_row", [1, Ltot])
    oT = kb.dout("oT", [128, Ltot])

    def const(name, shape):
        d_ = kb.din(name, shape)
        b = kb.sb(name, shape, F32)
        kb.dma("sync", b[:], d_[:], b, load=True)
        return b
    cw = [const("cw_q", [128, 4]), const("cw_k", [128, 4]), const("cw_v", [128, 4])]
    alog = const("alog", [1, 1])
    dtb = const("dtb", [1, 1])
    ident = const("ident", [128, 128])
    mLs = const("mLs", [64, W])
    mUs = const("mUs", [64, W])
    mUi = const("mUi", [64, W])
    I8 = const("I8", [64, W])
    rmask = const("rmask_row", [1, W])
    ones_row = const("ones_row", [1, 128])
    ones_f = kb.sb("ones_f", [128, 128], F32)
    kb.memset(V, ones_f, ones_f[:, :], 1.0)
    epsb = kb.sb("epsb", [128, 1], F32)
    kb.memset(V, epsb, epsb[:, :], EPS)
    nA = kb.sb("nA", [1, 1], F32)
    kb.act(nA[:, :], alog[:, :], AF.Exp, [alog], [nA])
    kb.ts(V, nA[:, :], nA[:, :], -1.0, ALU.mult, [nA], [nA])

    def new(name, shape, dtype=F32):
        return kb.sb(name, shape, dtype)

    def dbl(name, shape, dtype=F32, n=2):
        return Rot([kb.sb("%s%d" % (name, i), shape, dtype) for i in range(n)])
    xin = [dbl("xin%d" % i, [128, W + 3]) for i in range(3)]
    a_r = dbl("a_", [1, W])
    b_r = dbl("b_", [1, W])
    acc = [new("acc%d" % i, [128, W]) for i in range(3)]
    sq = new("sq", [128, W])
    rs = [new("rs%d" % i, [128, W]) for i in range(2)]
    qn = new("qn", [128, W])
    kn = new("kn", [128, W])
    knb = new("knb", [128, W], BF16)
    qnb = new("qnb", [128, W], BF16)
    qd = dbl("qd", [128, W], BF16)
    egc = dbl("egc", [128, W])
    rows = {n_: new("row_" + n_, [1, W]) for n_ in ("e", "g", "gc", "l2", "gcl", "ngc", "beta", "bg", "ed", "egc")}
    cols = dbl("cols", [64, 3 * NCH])
    vb = dbl("vb", [64, NCH, 128], BF16)
    kbg = dbl("kbg", [64, NCH, 128], BF16)
    kdec = dbl("kdec", [64, NCH, 128], BF16)
    M1 = new("M1", [64, W])
    M1T = new("M1T", [64, W])
    DT = new("DT", [64, W])
    Pm = [new("Pm%d" % i, [64, W]) for i in range(2)]
    PmT = [new("PmT%d" % i, [64, W]) for i in range(2)]
    RT = [new("RT%d" % i, [64, W]) for i in range(2)]
    attnT = dbl("attnT", [64, W], BF16)
    TTb = dbl("TTb", [64, W], BF16)
    nwT = dbl("nwT", [128, W], BF16)
    S = new("S", [128, 128])
    Sb = new("Sb", [128, 128], BF16)
    kb.memset(V, S, S[:, :], 0.0)
    kb.memset(V, Sb, Sb[:, :], 0.0)
    vnew = dbl("vnew", [64, 128], BF16, 3)
    ost = dbl("ost", [128, W], F32, 2)
    srcT = [qrT, krT, vrT]

    for s in range(Ltot // W):
        ts_ = slice(s * W, (s + 1) * W)
        x_ = [xin[i].next() for i in range(3)]
        for i in range(3):
            if s == 0:
                kb.memset(V, x_[i], x_[i][:, 0:3], 0.0)
                kb.dma("sync", x_[i][:, 3:W + 3], srcT[i][:, 0:W], x_[i], load=True)
            else:
                kb.dma("sync", x_[i][:, :], srcT[i][:, s * W - 3:(s + 1) * W], x_[i], load=True)
        a_ = a_r.next()
        b_ = b_r.next()
        kb.dma("sync", a_[:, :], a_d[:, ts_], a_, load=True)
        kb.dma("sync", b_[:, :], b_d[:, ts_], b_, load=True)
        for i in range(3):
            kb.ts(V, acc[i][:, :], x_[i][:, 0:W], cw[i][:, 0:1], ALU.mult, [x_[i], cw[i]], [acc[i]])
            for j in range(1, 4):
                kb.stt(acc[i][:, :], x_[i][:, j:j + W], cw[i][:, j:j + 1], acc[i][:, :], ALU.mult, ALU.add,
                       [x_[i], cw[i], acc[i]], [acc[i]])
            kb.act(acc[i][:, :], acc[i][:, :], AF.Silu, [acc[i]], [acc[i]])
        qc, kc, vc = acc
        for i, (src, dst, scl) in enumerate(((qc, qn, 128 ** -0.5), (kc, kn, 1.0))):
            kb.act(sq[:, :], src[:, :], AF.Square, [src], [sq])
            pss = misc.next()
            kb.mm(pss, pss[:, :W], ones_f[:, :], sq[:, :], True, True, [ones_f, sq])
            kb.act(rs[i][:, :], pss[:, :W], AF.Sqrt, [pss, epsb], [rs[i]], bias=epsb[:, 0:1])
            kb.op(V, lambda e, i=i: e.reciprocal(out=rs[i][:, :], in_=rs[i][:, :]), [rs[i]], [rs[i]])
            kb.stt(dst[:, :], src[:, :], scl, rs[i][:, :], ALU.mult, ALU.mult, [src, rs[i]], [dst])
        kb.copy(G, knb[:, :], kn[:, :], [kn], [knb])
        kb.copy(G, qnb[:, :], qn[:, :], [qn], [qnb])
        R_ = rows
        kb.act(R_["e"][:, :], a_[:, :], AF.Exp, [a_, dtb], [R_["e"]], bias=dtb[0:1, 0:1])
        kb.act(R_["e"][:, :], R_["e"][:, :], AF.Ln, [R_["e"]], [R_["e"]], bias=1.0)
        kb.ts(V, R_["g"][:, :], R_["e"][:, :], nA[0:1, 0:1], ALU.mult, [R_["e"], nA], [R_["g"]])
        kb.op(V, lambda e: e.tensor_tensor_scan(out=R_["gc"][:, :], data0=rmask[:, :], data1=R_["g"][:, :], initial=0.0,
                                                op0=ALU.mult, op1=ALU.add), [rmask, R_["g"]], [R_["gc"]])
        kb.act(R_["l2"][:, :], b_[:, :], AF.Exp, [b_], [R_["l2"]], scale=-1.0)
        kb.act(R_["l2"][:, :], R_["l2"][:, :], AF.Ln, [R_["l2"]], [R_["l2"]], bias=1.0)
        kb.tt(V, R_["gcl"][:, :], R_["gc"][:, :], R_["l2"][:, :], ALU.subtract, [R_["gc"], R_["l2"]], [R_["gcl"]])
        kb.ts(V, R_["ngc"][:, :], R_["gc"][:, :], -1.0, ALU.mult, [R_["gc"]], [R_["ngc"]])
        kb.act(R_["beta"][:, :], R_["l2"][:, :], AF.Exp, [R_["l2"]], [R_["beta"]], scale=-1.0)
        kb.act(R_["bg"][:, :], R_["gcl"][:, :], AF.Exp, [R_["gcl"]], [R_["bg"]])
        g3 = R_["gc"].ap.rearrange("p (c t) -> p c t", t=CH)
        e3 = R_["ed"].ap.rearrange("p (c t) -> p c t", t=CH)
        kb.tt(V, e3, g3[:, :, CH - 1:CH].to_broadcast([1, NCH, CH]), g3, ALU.subtract, [R_["gc"]], [R_["ed"]])
        kb.act(R_["ed"][:, :], R_["ed"][:, :], AF.Exp, [R_["ed"]], [R_["ed"]])
        kb.act(R_["egc"][:, :], R_["gc"][:, :], AF.Exp, [R_["gc"]], [R_["egc"]])
        pc = misc.next()
        for qi, rn in enumerate(("beta", "bg", "ed")):
            for n in range(NCH):
                kb.mm(pc, pc[:64, qi * NCH + n:qi * NCH + n + 1], R_[rn][0:1, n * CH:(n + 1) * CH], ones_row[0:1, 0:1],
                      True, True, [R_[rn], ones_row])
        cols_ = cols.next()
        kb.copy(V, cols_[:, :], pc[:64, :3 * NCH], [pc], [cols_])
        pb = misc.next()
        kb.mm(pb, pb[:, :W], ones_row[0:1, :], R_["egc"][0:1, :], True, True, [ones_row, R_["egc"]])
        egc_ = egc.next()
        kb.copy("scalar", egc_[:, :], pb[:, :W], [pb], [egc_])
        qd_ = qd.next()
        kb.tt(V, qd_[:, :], qn[:, :], egc_[:, :], ALU.mult, [qn, egc_], [qd_])
        vb_, kbg_, kdec_ = vb.next(), kbg.next(), kdec.next()
        for (src, outs) in ((kn, ((kbg_, 1), (kdec_, 2))), (vc, ((vb_, 0),))):
            for hb in range(2):
                pt = misc.next()
                for cc in range(4):
                    n = hb * 4 + cc
                    kb.op("tensor", lambda e, pt=pt, cc=cc, n=n, src=src: e.transpose(
                        out=pt[:64, cc * 128:(cc + 1) * 128], in_=src[:, n * CH:(n + 1) * CH], identity=ident[:, :]),
                        [src, ident], [pt])
                for (dst, qi) in outs:
                    cb = cols_.ap[:, qi * NCH + hb * 4:qi * NCH + hb * 4 + 4].unsqueeze(2).to_broadcast([64, 4, 128])
                    kb.tt(V, dst.ap[:, hb * 4:hb * 4 + 4, :], pt[:64, :].rearrange("p (c d) -> p c d", d=128), cb,
                          ALU.mult, [pt, cols_], [dst])
        pG, pQK, pE1, pE1T = misc.next(), misc.next(), misc.next(), misc.next()
        for n in range(NCH):
            cs = slice(n * CH, (n + 1) * CH)
            kb.mm(pG, pG[:64, cs], knb[:, cs], knb[:, cs], True, True, [knb])
            kb.mm(pQK, pQK[:64, cs], knb[:, cs], qnb[:, cs], True, True, [knb, qnb])
            kb.mm(pE1, pE1[:64, cs], R_["gcl"][0:1, cs], ones_row[0:1, 0:CH], True, False, [R_["gcl"], ones_row])
            kb.mm(pE1, pE1[:64, cs], ones_row[0:1, 0:CH], R_["ngc"][0:1, cs], False, True, [R_["ngc"], ones_row])
            kb.mm(pE1T, pE1T[:64, cs], ones_row[0:1, 0:CH], R_["gcl"][0:1, cs], True, False, [R_["gcl"], ones_row])
            kb.mm(pE1T, pE1T[:64, cs], R_["ngc"][0:1, cs], ones_row[0:1, 0:CH], False, True, [R_["ngc"], ones_row])
        kb.tt(V, M1[:, :], pE1[:64, :W], mLs[:, :], ALU.min, [pE1, mLs], [M1])
        kb.act(M1[:, :], M1[:, :], AF.Exp, [M1], [M1])
        kb.tt(V, M1T[:, :], pE1T[:64, :W], mUs[:, :], ALU.min, [pE1T, mUs], [M1T])
        kb.act(M1T[:, :], M1T[:, :], AF.Exp, [M1T], [M1T])
        kb.stt(Pm[0][:, :], pG[:64, :W], -1.0, M1[:, :], ALU.mult, ALU.mult, [pG, M1], [Pm[0]])
        kb.stt(PmT[0][:, :], pG[:64, :W], -1.0, M1T[:, :], ALU.mult, ALU.mult, [pG, M1T], [PmT[0]])
        kb.tt(G, RT[0][:, :], PmT[0][:, :], I8[:, :], ALU.add, [PmT[0], I8], [RT[0]])
        pE2T = misc.next()
        for n in range(NCH):
            cs = slice(n * CH, (n + 1) * CH)
            kb.mm(pE2T, pE2T[:64, cs], ones_row[0:1, 0:CH], R_["gc"][0:1, cs], True, False, [R_["gc"], ones_row])
            kb.mm(pE2T, pE2T[:64, cs], R_["ngc"][0:1, cs], ones_row[0:1, 0:CH], False, True, [R_["ngc"], ones_row])
        kb.tt(V, DT[:, :], pE2T[:64, :W], mUi[:, :], ALU.min, [pE2T, mUi], [DT])
        kb.act(DT[:, :], DT[:, :], AF.Exp, [DT], [DT])
        at_ = attnT.next()
        kb.tt(V, at_[:, :], pQK[:64, :W], DT[:, :], ALU.mult, [pQK, DT], [at_])
        cur = 0
        for j in range(1, nst):
            nxt = 1 - cur
            pP = misc.next()
            for n in range(NCH):
                cs = slice(n * CH, (n + 1) * CH)
                kb.mm(pP, pP[:64, cs], PmT[cur][:, cs], Pm[cur][:, cs], True, True, [PmT[cur], Pm[cur]])
            if j < 5:
                pPT = misc.next()
                for n in range(NCH):
                    cs = slice(n * CH, (n + 1) * CH)
                    kb.mm(pPT, pPT[:64, cs], Pm[cur][:, cs], PmT[cur][:, cs], True, True, [PmT[cur], Pm[cur]])
            kb.copy(V, Pm[nxt][:, :], pP[:64, :W], [pP], [Pm[nxt]])
            if j < 5:
                kb.copy("scalar", PmT[nxt][:, :], pPT[:64, :W], [pPT], [PmT[nxt]])
            pR = misc.next()
            for n in range(NCH):
                cs = slice(n * CH, (n + 1) * CH)
                kb.mm(pR, pR[:64, cs], Pm[nxt][:, cs], RT[cur][:, cs], True, True, [Pm[nxt], RT[cur]])
            kb.tt(V, RT[nxt][:, :], RT[cur][:, :], pR[:64, :W], ALU.add, [RT[cur], pR], [RT[nxt]])
            cur = nxt
        TT_ = TTb.next()
        kb.copy(V, TT_[:, :], RT[cur][:, :], [RT[cur]], [TT_])
        pW = misc.next()
        for n in range(NCH):
            cs = slice(n * CH, (n + 1) * CH)
            kb.mm(pW, pW[:, cs], kbg_[:, n, :], TT_[:, cs], True, True, [kbg_, TT_])
        nw_ = nwT.next()
        kb.act(nw_[:, :], pW[:, :W], AF.Copy, [pW], [nw_], scale=-1.0)
        po = po_r.next()
        pv, pS = P["v"], P["S"]
        for n in range(NCH):
            cs = slice(n * CH, (n + 1) * CH)
            kb.mm(pv, pv[:64, :128], TT_[:, cs], vb_[:, n, :], True, False, [TT_, vb_])
            kb.mm(pv, pv[:64, :128], nw_[:, cs], Sb[:, :], False, True, [nw_, Sb])
            vn = vnew.next()
            kb.copy("scalar", vn[:, :], pv[:64, :128], [pv], [vn])
            kb.mm(po, po[:, cs], Sb[:, :], qd_[:, cs], True, False, [Sb, qd_])
            kb.mm(po, po[:, cs], vn[:, :], at_[:, cs], False, True, [vn, at_])
            kb.mm(pS, pS[:, :128], kdec_[:, n, :], vn[:, :], True, True, [kdec_, vn])
            gcol = egc_[:, n * CH + CH - 1:n * CH + CH]
            kb.stt(Sb[:, :], S[:, :], gcol, pS[:, :128], ALU.mult, ALU.add, [S, egc_, pS], [Sb])
            kb.stt(S[:, :], S[:, :], gcol, pS[:, :128], ALU.mult, ALU.add, [S, egc_, pS], [S])
        o_ = ost.next()
        kb.copy(V, o_[:, :], po[:, :W], [po], [o_])
        kb.dma("sync", oT[:, ts_], o_[:, :], o_, load=False)
        if debug and s == 0:
            for nm, bf in (("qn", qn), ("kn", kn), ("vc", vc), ("gc", R_["gc"]), ("beta", R_["beta"]), ("M1", M1), ("M1T", M1T),
                           ("DT", DT), ("TT", RT[cur]), ("cols", cols_), ("egc", egc_), ("Pm0", Pm[0]), ("Pm1", Pm[1]), ("PmT0", PmT[0]), ("PmT1", PmT[1]), ("RT0", RT[0]), ("RT1", RT[1])):
                shp = [int(x) for x in bf.ap.shape]
                dd_ = kb.dout("dbg_" + nm, shp)
                kb.dma("sync", dd_[:], bf[:], bf, load=False)
    kb.finish()
    return kb.nc


def gdn_consts():
    i = np.arange(64)
    Ls = np.where(i[:, None] > i[None, :], 0.0, NEG_BIG).astype(np.float32)
    Us = np.where(i[:, None] < i[None, :], 0.0, NEG_BIG).astype(np.float32)
    Ui = np.where(i[:, None] <= i[None, :], 0.0, NEG_BIG).astype(np.float32)
    rm = np.ones((1, 512), np.float32)
    rm[:, ::64] = 0.0
    return {"ident": np.eye(128, dtype=np.float32), "mLs": np.tile(Ls, (1, 8)), "mUs": np.tile(Us, (1, 8)),
            "mUi": np.tile(Ui, (1, 8)), "I8": np.tile(np.eye(64, dtype=np.float32), (1, 8)), "rmask_row": rm,
            "ones_row": np.ones((1, 128), np.float32)}


def chunk_layout(v):
    v = np.asarray(v, np.float32).reshape(-1, 128)
    return np.ascontiguousarray(v.T)


def run(nc, in_maps):
    res = run_bass_kernel_spmd(nc, in_maps, core_ids=list(range(NCORES)))
    return res.results


def _c(a):
    return np.ascontiguousarray(a, dtype=np.float32)


def kernel(**inp):
    inp = {k: np.asarray(v) for k, v in inp.items()}
    NT = L // NCORES
    x = inp["x"][0]
    xT = _c(x.T)
    cl = chunk_layout
    ada_w = np.concatenate([inp["ada_w0"], inp["ada_w1"]], axis=1)
    ada_b = np.concatenate([inp["ada_b0"], inp["ada_b1"]])
    res = run(build_L0(), [{"c": cl(inp["c"][0]), "ada_w": _c(ada_w[:, i * 3072:(i + 1) * 3072]),
                            "ada_b": cl(ada_b[i * 3072:(i + 1) * 3072])} for i in range(NCORES)])
    mod = np.concatenate([r["mod"].T.reshape(-1) for r in res])
    del ada_w
    mod0, mod1 = cl(mod[:12288]), cl(mod[12288:])
    w_in0 = _c(inp["w_in0"])
    res = run(build_LA(NT, 4112), [{"xT": _c(xT[:, i * NT:(i + 1) * NT]), "modT": mod0, "wpre": cl(inp["mix_pre0"]),
                                    "w_in": w_in0} for i in range(NCORES)])
    PT0 = np.concatenate([r["PT"] for r in res], axis=1)
    res = run(build_LS5(L), [dict(uT=_c(PT0[i * 128:(i + 1) * 128]), **s5_host_params(inp, i)) for i in range(NCORES)])
    yS5T = np.concatenate([r["yT"] for r in res], axis=0)
    cwT = _c(inp["gdn_conv_w"].T)
    gc_ = gdn_consts()
    ins = []
    for i in range(NCORES):
        h = i // 2
        d = {"qrT": _c(PT0[1024 + h * 128:1024 + (h + 1) * 128]), "krT": _c(PT0[1536 + h * 128:1536 + (h + 1) * 128]),
             "vrT": _c(PT0[2048 + i * 128:2048 + (i + 1) * 128]), "a_row": _c(PT0[4096 + i:4097 + i]),
             "b_row": _c(PT0[4104 + i:4105 + i]),
             "cw_q": _c(cwT[h * 128:(h + 1) * 128]), "cw_k": _c(cwT[512 + h * 128:512 + (h + 1) * 128]),
             "cw_v": _c(cwT[1024 + i * 128:1024 + (i + 1) * 128]),
             "alog": _c(inp["gdn_a_log"][i].reshape(1, 1)), "dtb": _c(inp["gdn_dt_bias"][i].reshape(1, 1))}
        d.update(gc_)
        ins.append(d)
    res = run(build_LG(L), ins)
    ogT = np.concatenate([r["oT"] for r in res], axis=0)
    zT = PT0[3072:4096]
    common = {"modT": mod0, "wpost_m": cl(inp["mix_post0"]), "wpre_f": cl(inp["ffn_pre0"]), "wpost_f": cl(inp["ffn_post0"]),
              "w_out": _c(inp["w_out0"]), "ffn_gate": _c(inp["ffn_gate0"]), "ffn_up": _c(inp["ffn_up0"]),
              "ffn_down": _c(inp["ffn_down0"]), "glu_w": _c(inp["s5_glu_w"]), "glu_b": cl(inp["s5_glu_b"]),
              "gnw": _c(inp["gdn_norm_w"].reshape(128, 1)), "modT_n": mod1, "wpre_n": cl(inp["mix_pre1"]),
              "w_in_n": _c(inp["w_in1"])}
    ins = []
    for i in range(NCORES):
        sl = slice(i * NT, (i + 1) * NT)
        d = {"xT": _c(xT[:, sl]), "yS5T": _c(yS5T[:, sl]), "ogT": _c(ogT[:, sl]), "zT": _c(zT[:, sl])}
        d.update(common)
        ins.append(d)
    res = run(build_LC(NT, 0, 6160), ins)
    x1T = np.concatenate([r["xoutT"] for r in res], axis=1)
    PT1 = np.concatenate([r["PTn"] for r in res], axis=1)
    del PT0, yS5T, ogT, common, ins
    gla_c = gla_consts()
    ins = []
    for i in range(NCORES):
        h, e = i // 2, i % 2
        d = {"qT": _c(PT1[h * 256:(h + 1) * 256]), "kT": _c(PT1[1024 + h * 256:1024 + (h + 1) * 256]),
             "vT": _c(PT1[2048 + i * 256:2048 + (i + 1) * 256]), "glT": _c(PT1[6144:6160]),
             "w2": _c(inp["gla_gate_w2"][:, h * 256:(h + 1) * 256]), "gb": cl(inp["gla_gate_b"][h * 256:(h + 1) * 256])}
        d.update(gla_c)
        ins.append(d)
    res = run(build_LD(L), ins)
    oT = np.concatenate([r["oT"] for r in res], axis=0)
    rT = PT1[4096:6144]
    common = {"modT": mod1, "wpost_m": cl(inp["mix_post1"]), "wpre_f": cl(inp["ffn_pre1"]), "wpost_f": cl(inp["ffn_post1"]),
              "w_out": _c(inp["w_out1"]), "ffn_gate": _c(inp["ffn_gate1"]), "ffn_up": _c(inp["ffn_up1"]),
              "ffn_down": _c(inp["ffn_down1"]), "gnw4": cl(inp["gla_norm_w"])}
    ins = []
    for i in range(NCORES):
        sl = slice(i * NT, (i + 1) * NT)
        d = {"xT": _c(x1T[:, sl]), "oT": _c(oT[:, sl]), "rT": _c(rT[:, sl])}
        d.update(common)
        ins.append(d)
    res = run(build_LC(NT, 1, 0), ins)
    x2T = np.concatenate([r["xoutT"] for r in res], axis=1)
    return np.ascontiguousarray(x2T.T)[None].astype(np.float32)
```

```python
import numpy as np
import concourse.bass as bass
import concourse.mybir as mybir
from concourse.bass_utils import run_bass_kernel_spmd

F32 = mybir.dt.float32
BF16 = mybir.dt.bfloat16
ALU = mybir.AluOpType
AF = mybir.ActivationFunctionType
AX = mybir.AxisListType
U8 = mybir.dt.uint8
I32 = mybir.dt.int32

NCORES = 8
D = 2048
L = 16384
EPS = 1e-6
FFN_H = 5632


ARENA = 207 * 1024
ENGS = ("tensor", "vector", "scalar", "gpsimd", "sync")
_ISZ = {F32: 4, BF16: 2, I32: 4, U8: 1}


class _Eng:
    def __init__(self, name, sem):
        self.name, self.sem = name, sem
        self.cnt = 0
        self.seen = {}
        self.prog = []


class Buf:
    def __init__(self, kb, name, ap):
        self.kb, self.name, self.ap = kb, name, ap
        self.w = {}
        self.r = {}
        self.dsem = None
        self.dcnt = 0
        self.dw = []
        self.dr = []

    def __getitem__(self, k):
        return self.ap[k]


class KB:
    def __init__(self):
        self.nc = bass.Bass("TRN2", target_bir_lowering=False)
        nc = self.nc
        self.E = {n: _Eng(n, nc.alloc_semaphore("es_" + n)) for n in ENGS}
        self.cc_sem = nc.alloc_semaphore("cc_sem")
        self.cc_cnt = 0
        self.arena = nc.alloc_sbuf_tensor("arena", [128, ARENA], U8)
        self.sp = 0
        self.marks = []
        self.bufs = []
        self.sem_pool = []
        self.ps_list = []
        for i in range(8):
            t = nc.alloc_psum_tensor("ps%d" % i, [128, 512], F32)
            self.ps_list.append(Buf(self, "ps%d" % i, t[:]))
        self.ps_i = 0
        self._uid = 0
        self.all_dsems = {}

    def sb(self, name, shape, dtype):
        shape = list(shape)
        n = 1
        for s_ in shape[1:]:
            n *= s_
        nbytes = (n * _ISZ[dtype] + 31) // 32 * 32
        off = self.sp
        self.sp += nbytes
        assert self.sp <= ARENA, "SBUF arena overflow at %s (%d)" % (name, self.sp)
        ap = self.arena[0:shape[0], off:off + n * _ISZ[dtype]]
        if dtype != U8:
            ap = ap.bitcast(dtype)
        if len(shape) == 3:
            ap = ap.rearrange("p (a b) -> p a b", b=shape[2])
        b = Buf(self, name, ap)
        self.bufs.append(b)
        return b

    def view(self, name, ap):
        b = Buf(self, name, ap)
        self.bufs.append(b)
        return b

    def next_ps(self):
        b = self.ps_list[self.ps_i % 8]
        self.ps_i += 1
        return b

    def din(self, name, shape, dtype=F32):
        if not hasattr(self, "_din"):
            self._din = {}
        if name not in self._din:
            self._din[name] = self.nc.dram_tensor(name, list(shape), dtype, kind="ExternalInput").ap()
        return self._din[name]

    def dout(self, name, shape, dtype=F32):
        return self.nc.dram_tensor(name, list(shape), dtype, kind="ExternalOutput").ap()

    def dscratch(self, name, shape, dtype=F32):
        return self.nc.dram_tensor(name, list(shape), dtype)

    def phase_begin(self):
        self.marks.append((self.sp, len(self.bufs)))

    def phase_end(self, collective=None, wait=True):
        self.barrier()
        if collective is not None:
            colls = collective if isinstance(collective, (list, tuple)) else [collective]
            e = self.E["gpsimd"]
            for c_ in colls:
                self.cc_cnt += 1
                e.prog.append(("c", c_, self.cc_sem))
            if wait:
                for n in ENGS:
                    self.E[n].prog.append(("w", self.cc_sem, self.cc_cnt))
        sp, nb = self.marks.pop()
        for b in self.bufs[nb:]:
            if b.dsem is not None:
                self.sem_pool.append((b.dsem, b.dcnt))
        del self.bufs[nb:]
        self.sp = sp

    def barrier(self):
        targets = []
        for n in ENGS:
            if self.E[n].cnt:
                targets.append((("e", n), self.E[n].sem, self.E[n].cnt))
        for k, (sem, cnt) in self.all_dsems.items():
            if cnt:
                targets.append((("d", k), sem, cnt * 16))
        for n in ENGS:
            self._emit_waits(self.E[n], [t for t in targets if t[0] != ("e", n)])

    def _needs(self, ename, R, W):
        need = []
        for b in R:
            for en, c in b.w.items():
                if not (en == ename and ename == "tensor"):
                    need.append((("e", en), self.E[en].sem, c))
            for sem, val in b.dw:
                need.append((("d", id(sem)), sem, val))
        for b in W:
            for en, c in b.w.items():
                if en != ename:
                    need.append((("e", en), self.E[en].sem, c))
            for en, c in b.r.items():
                if en != ename:
                    need.append((("e", en), self.E[en].sem, c))
            for sem, val in b.dw + b.dr:
                need.append((("d", id(sem)), sem, val))
        return need

    def _emit_waits(self, e, need):
        best = {}
        for key, sem, val in need:
            if key not in best or best[key][1] < val:
                best[key] = (sem, val)
        for key, (sem, val) in best.items():
            if e.seen.get(key, 0) < val:
                e.prog.append(("w", sem, val))
                e.seen[key] = val

    def op(self, ename, fn, R=(), W=()):
        e = self.E[ename]
        self._emit_waits(e, self._needs(ename, R, W))
        e.prog.append(("i", fn, e.sem))
        e.cnt += 1
        for b in R:
            b.r[ename] = e.cnt
        for b in W:
            b.w = {ename: e.cnt}
            b.r = {}
            b.dw = []
            b.dr = []

    def dma(self, q, out, in_, sbuf, load, R=(), W=(), group=False):
        e = self.E[q]
        R = list(R)
        W = list(W)
        if load:
            if group:
                sbuf.dw = []
            W.append(sbuf)
        else:
            R.append(sbuf)
        self._emit_waits(e, self._needs("dma", R, W))
        if sbuf.dsem is None:
            if self.sem_pool:
                sbuf.dsem, sbuf.dcnt = self.sem_pool.pop()
            else:
                sbuf.dsem = self.nc.alloc_semaphore("ds_%d" % self._uid)
                self._uid += 1
        e.prog.append(("d", (lambda eng, out=out, in_=in_: eng.dma_start(
            out=out(self.cur_pid) if callable(out) else out, in_=in_(self.cur_pid) if callable(in_) else in_)), sbuf.dsem))
        sbuf.dcnt += 1
        self.all_dsems[id(sbuf.dsem)] = [sbuf.dsem, sbuf.dcnt]
        ev = (sbuf.dsem, sbuf.dcnt * 16)
        for b in W:
            b.w = {}
            b.r = {}
            b.dw = [ev]
            b.dr = []
        for b in R:
            b.dr = [x for x in b.dr if x[0] is not ev[0]] + [ev]

    def dram_copy_dyn(self, name, dst_t, src_ap_fn):
        b = self.view(name, dst_t.ap())
        self.dma("sync", dst_t.ap(), src_ap_fn, b, load=True)
        self._emit_waits(self.E["sync"], [(("d", id(sem)), sem, val) for sem, val in b.dw])

    def build(self):
        self.barrier()
        nc = self.nc
        with nc.Block() as block:
            for n in ENGS:
                def section(eng, items=self.E[n].prog):
                    self.cur_pid = eng.snap(eng.partition_id() * self.tok_per_core, min_val=0, max_val=7 * self.tok_per_core)
                    for it in items:
                        if it[0] == "w":
                            eng.wait_ge(it[1], it[2])
                        elif it[0] == "i":
                            it[1](eng).then_inc(it[2], 1)
                        elif it[0] == "d":
                            it[1](eng).then_inc(it[2], 16)
                        else:
                            it[1](eng).then_inc(it[2])
                getattr(block, n)(section)
        return nc

    def mm(self, ps, out, lhsT, rhs, start, stop, R):
        return self.op("tensor", lambda eng: eng.matmul(out, lhsT, rhs, start=start, stop=stop), R, [ps])

    def act(self, out, in_, func, R, W, bias=None, scale=1.0):
        kw = dict(out=out, in_=in_, func=func, scale=scale)
        if bias is not None:
            kw["bias"] = bias
        return self.op("scalar", lambda e: e.activation(**kw), R, W)

    def tt(self, eng, out, in0, in1, op, R, W):
        return self.op(eng, lambda e: e.tensor_tensor(out=out, in0=in0, in1=in1, op=op), R, W)

    def stt(self, out, in0, scalar, in1, op0, op1, R, W):
        return self.op("vector", lambda e: e.scalar_tensor_tensor(out=out, in0=in0, scalar=scalar, in1=in1,
                                                                   op0=op0, op1=op1), R, W)

    def ts(self, eng, out, in0, s1, op0, R, W, s2=None, op1=None):
        if op1 is None:
            return self.op(eng, lambda e: e.tensor_scalar(out=out, in0=in0, scalar1=s1, scalar2=None, op0=op0), R, W)
        return self.op(eng, lambda e: e.tensor_scalar(out=out, in0=in0, scalar1=s1, scalar2=s2, op0=op0, op1=op1), R, W)

    def copy(self, eng, out, in_, R, W):
        if eng == "scalar":
            return self.op("scalar", lambda e: e.copy(out=out, in_=in_), R, W)
        return self.op(eng, lambda e: e.tensor_copy(out=out, in_=in_), R, W)

    def memset(self, eng, buf, ap, val):
        return self.op(eng, lambda e: e.memset(ap, val), [], [buf])


T = 1024
HT = 512


class Dense:
    def __init__(self, kb, n_wb=4, n_ost=4):
        self.kb = kb
        self.n_ost = n_ost
        self.ones = kb.sb("ones_bf", [128, 128], BF16)
        kb.memset("vector", self.ones, self.ones[:], 1.0)
        self.R1t = kb.sb("R1", [128, 16, T], F32)
        self.R1 = [kb.view("R1_%d" % i, self.R1t.ap[:, i, :]) for i in range(16)]
        self.R3t = kb.sb("R3", [128, 16, T], BF16)
        self.R3 = [kb.view("R3_%d" % i, self.R3t.ap[:, i, :]) for i in range(16)]
        self.wb = [kb.sb("wb%d" % i, [128, 4096], BF16) for i in range(n_wb)]
        self.wi = 0
        self.sq = [kb.sb("sq%d" % i, [128, T], BF16) for i in range(2)]
        self.tmp = [kb.sb("tmp%d" % i, [128, T], F32) for i in range(2)]
        self.rstd = kb.sb("rstd", [128, T], F32)
        self.ost = [kb.sb("ost%d" % i, [128, HT], F32) for i in range(n_ost)]
        self.oi = 0
        self.evi = 0

    def next_wb(self):
        b = self.wb[self.wi % len(self.wb)]
        self.wi += 1
        return b

    def load_w(self, Wd, k0, k1, c0, nb):
        kb = self.kb
        wb = self.next_wb()
        kc = k1 - k0
        view = wb.ap[:, :kc * nb].rearrange("p (k n) -> p k n", n=nb)
        src = Wd.rearrange("(k p) n -> p k n", p=128)[:, k0:k1, c0:c0 + nb]
        kb.dma("gpsimd", view, src, wb, load=True)
        return wb, view

    def sumsq_rstd(self, src, nfeat_chunks, scale, out_rstd, width=T):
        kb = self.kb
        nh = width // HT
        pss = [kb.next_ps() for _ in range(nh)]
        n = len(src)
        for kc in range(n):
            sq = self.sq[kc % 2]
            kb.act(sq[:, :width], src[kc][:, :width], AF.Square, [src[kc]], [sq])
            for h in range(nh):
                kb.mm(pss[h], pss[h][:, :HT], self.ones[:, :], sq[:, h * HT:(h + 1) * HT], kc == 0, kc == n - 1,
                      [self.ones, sq])
        for h in range(nh):
            kb.act(out_rstd[:, h * HT:(h + 1) * HT], pss[h][:, :HT], AF.Sqrt, [pss[h]], [out_rstd], bias=self.epsb[:, 0:1],
                   scale=scale)
        kb.op("vector", lambda e: e.reciprocal(out=out_rstd[:, :width], in_=out_rstd[:, :width]), [out_rstd], [out_rstd])

    def consts(self):
        kb = self.kb
        self.epsb = kb.sb("epsb", [128, 1], F32)
        kb.memset("vector", self.epsb, self.epsb[:], EPS)

    def prenorm(self, src, gs, sh, dst):
        kb = self.kb
        self.sumsq_rstd(src, 16, 1.0 / D, self.rstd)
        for kc in range(16):
            tmp = self.tmp[kc % 2]
            kb.stt(tmp[:, :], src[kc][:, :], gs[:, kc:kc + 1], self.rstd[:, :], ALU.mult, ALU.mult,
                   [src[kc], gs, self.rstd], [tmp])
            kb.act(dst[kc][:, :], tmp[:, :], AF.Identity, [tmp, sh], [dst[kc]], bias=sh[:, kc:kc + 1])

    def proj(self, hsrc, Wd, ncols, epi, kgroups=None, nb=256, c_base=0):
        kb = self.kb
        KC = len(hsrc)
        if kgroups is None:
            kgroups = [(0, KC)]
        c = 0
        while c < ncols:
            cb = min(nb, ncols - c)
            views = []
            for (k0, k1) in kgroups:
                wbuf, v = self.load_w(Wd, k0, k1, c_base + c, cb)
                views.append((wbuf, v, k0, k1))
            m = 0
            while m < cb:
                mc = min(128, cb - m)
                for half in range(T // HT):
                    ps = kb.next_ps()
                    first = True
                    for (wbuf, v, k0, k1) in views:
                        for k in range(k0, k1):
                            kb.mm(ps, ps[:mc, :HT], v[:, k - k0, m:m + mc], hsrc[k][:, half * HT:(half + 1) * HT],
                                  first, (k == KC - 1), [wbuf, hsrc[k]])
                            first = False
                    epi(c + m, mc, half, ps)
                m += 128
            c += cb

    def evac_engine(self):
        self.evi += 1
        return "scalar" if self.evi % 2 else "vector"

    def epi_store(self, out_dram, t0, track=None):
        def epi(col, mc, half, ps):
            st = self.ost[self.oi % self.n_ost]
            self.oi += 1
            self.kb.copy(self.evac_engine(), st[:mc, :], ps[:mc, :HT], [ps], [st])
            self.kb.dma("sync", out_dram[col:col + mc, t0 + half * HT:t0 + (half + 1) * HT], st[:mc, :], st, load=False,
                        W=[track[col // 128]] if track else [])
        return epi


    def alloc_ffn(self):
        kb = self.kb
        self.Ht = kb.sb("H", [128, 22, T], BF16)
        self.H = [kb.view("H_%d" % i, self.Ht.ap[:, i, :]) for i in range(22)]
        self.xs = [kb.sb("xs%d" % i, [128, T], F32) for i in range(2)]

    def postnorm_residual(self, gw, res_src, res_bufs, dst, dst_bufs):
        kb = self.kb
        self.sumsq_rstd(self.R1, 16, 1.0 / D, self.rstd)
        for kc in range(16):
            xs = self.xs[kc % 2]
            kb.dma("sync", xs[:, :], res_src(kc), xs, load=True, R=[res_bufs[kc]] if res_bufs else [])
            tmp = self.tmp[kc % 2]
            r1 = self.R1[kc]
            kb.stt(tmp[:, :], r1[:, :], gw[:, kc:kc + 1], self.rstd[:, :], ALU.mult, ALU.mult, [r1, gw, self.rstd], [tmp])
            kb.tt("vector", r1[:, :], tmp[:, :], xs[:, :], ALU.add, [tmp, xs], [r1])
            kb.dma("sync", dst(kc), r1[:, :], r1, load=False, W=[dst_bufs[kc]] if dst_bufs else [])

    def ffn(self, Wg, Wu, Wd):
        kb = self.kb
        HH = FFN_H // 2
        wgv = Wg.rearrange("(k p) n -> p k n", p=128)
        wuv = Wu.rearrange("(k p) n -> p k n", p=128)
        for hh in range(2):
            for blk in range(22):
                col = hh * HH + blk * 128
                wb = self.next_wb()
                vg = wb.ap[:, 0:2048].rearrange("p (k n) -> p k n", n=128)
                vu = wb.ap[:, 2048:4096].rearrange("p (k n) -> p k n", n=128)
                kb.dma("gpsimd", vg, wgv[:, :, col:col + 128], wb, load=True)
                kb.dma("gpsimd", vu, wuv[:, :, col:col + 128], wb, load=True, group=True)
                for half in range(T // HT):
                    pg = kb.next_ps()
                    pu = kb.next_ps()
                    hs = slice(half * HT, (half + 1) * HT)
                    for k in range(16):
                        kb.mm(pg, pg[:, :HT], vg[:, k, :], self.R3[k][:, hs], k == 0, k == 15, [wb, self.R3[k]])
                    for k in range(16):
                        kb.mm(pu, pu[:, :HT], vu[:, k, :], self.R3[k][:, hs], k == 0, k == 15, [wb, self.R3[k]])
                    sg = self.tmp[half % 2]
                    kb.act(sg[:, :HT], pg[:, :HT], AF.Silu, [pg], [sg])
                    kb.tt("vector", self.H[blk][:, hs], sg[:, :HT], pu[:, :HT], ALU.mult, [sg, pu], [self.H[blk]])

            def epi(col, mc, half, ps, hh=hh):
                m = col // 128
                hs = slice(half * HT, (half + 1) * HT)
                r1 = self.R1[m]
                if hh == 0:
                    kb.copy(self.evac_engine(), r1[:, hs], ps[:, :HT], [ps], [r1])
                else:
                    kb.tt("vector", r1[:, hs], r1[:, hs], ps[:, :HT], ALU.add, [r1, ps], [r1])
            self.proj(self.H, Wd[hh * HH:(hh + 1) * HH, :], D, epi, nb=128)

    def epi_to_R1(self):
        def epi(col, mc, half, ps):
            m = col // 128
            r1 = self.R1[m]
            self.kb.copy(self.evac_engine(), r1[:, half * HT:(half + 1) * HT], ps[:, :HT], [ps], [r1])
        return epi

    def tail0(self, ysrc, osrc, zsrc, zbufs, glu_w, glu_b, gnw, t0):
        kb = self.kb
        for kc in range(8):
            kb.dma("sync", self.R1[kc][:, :], ysrc(kc, t0), self.R1[kc], load=True)
            kb.copy("scalar" if kc % 2 else "vector", self.H[kc][:, :], self.R1[kc][:, :], [self.R1[kc]], [self.H[kc]])

        def epi_glu(col, mc, half, ps):
            m = col // 128
            hs = slice(half * HT, (half + 1) * HT)
            sg = self.tmp[half % 2]
            kb.act(sg[:, :HT], ps[:, :HT], AF.Sigmoid, [ps, glu_b], [sg], bias=glu_b[:, m:m + 1])
            kb.tt("vector", self.R3[m][:, hs], self.R1[m][:, hs], sg[:, :HT], ALU.mult, [self.R1[m], sg], [self.R3[m]])
        self.proj(self.H[0:8], glu_w, 1024, epi_glu)
        for hd in range(8):
            ob = self.R1[8 + hd]
            zb = self.R1[hd]
            kb.dma("sync", ob[:, :], osrc(hd, t0), ob, load=True)
            kb.dma("sync", zb[:, :], zsrc(hd, t0), zb, load=True, R=[zbufs[hd]])
            self.sumsq_rstd([ob], 1, 1.0 / 128, self.rstd)
            t1 = self.tmp[0]
            t2 = self.tmp[1]
            kb.tt("vector", t1[:, :], ob[:, :], self.rstd[:, :], ALU.mult, [ob, self.rstd], [t1])
            kb.act(t2[:, :], zb[:, :], AF.Silu, [zb], [t2])
            kb.stt(self.R3[8 + hd][:, :], t1[:, :], gnw[:, 0:1], t2[:, :], ALU.mult, ALU.mult, [t1, gnw, t2],
                   [self.R3[8 + hd]])

    def tail1(self, osrc, rsrc, gnw4, t0):
        kb = self.kb
        for hd in range(4):
            cs = list(range(4 * hd, 4 * hd + 4))
            for c in cs:
                kb.dma("sync", self.R1[c][:, :], osrc(c, t0), self.R1[c], load=True)
            self.sumsq_rstd([self.R1[c] for c in cs], 4, 1.0 / 512, self.rstd)
            for j, c in enumerate(cs):
                xs = self.xs[j % 2]
                kb.dma("sync", xs[:, :], rsrc(c, t0), xs, load=True)
                t1 = self.tmp[0]
                t2 = self.tmp[1]
                kb.act(t2[:, :], xs[:, :], AF.Silu, [xs], [t2])
                kb.stt(t1[:, :], self.R1[c][:, :], gnw4[:, j:j + 1], self.rstd[:, :], ALU.mult, ALU.mult,
                       [self.R1[c], gnw4, self.rstd], [t1])
                kb.tt("vector", self.R3[c][:, :], t1[:, :], t2[:, :], ALU.mult, [t1, t2], [self.R3[c]])

    def load_x(self, xsrc, t0):
        for kc in range(16):
            self.kb.dma("sync", self.R1[kc][:, :], xsrc(kc, t0), self.R1[kc], load=True)

    def mod_consts(self, modT_d, wpre_d, wpost_d, which, tag=""):
        kb = self.kb
        if not hasattr(self, "modT" + tag):
            setattr(self, "modT" + tag, kb.sb("modT" + tag, [128, 96], F32))
            m_ = getattr(self, "modT" + tag)
            kb.dma("sync", m_.ap.rearrange("p (r j) -> p r j", j=24), modT_d, m_, load=True)
        self.modT = getattr(self, "modT" + tag)
        which = which + tag
        o = 0 if which[0] == "m" else 48
        res = {}
        if wpre_d is not None:
            wpre = kb.sb("wpre_" + which, [128, 16], F32)
            kb.dma("sync", wpre[:, :], wpre_d[:, :], wpre, load=True)
            gs = kb.sb("gs_" + which, [128, 16], F32)
            kb.stt(gs[:, :], self.modT[:, o + 16:o + 32], 1.0, wpre[:, :], ALU.add, ALU.mult, [self.modT, wpre], [gs])
            sh = kb.sb("sh_" + which, [128, 16], F32)
            kb.copy("vector", sh[:, :], self.modT[:, o:o + 16], [self.modT], [sh])
            res["gs"] = gs
            res["sh"] = sh
        if wpost_d is not None:
            wpost = kb.sb("wpost_" + which, [128, 16], F32)
            kb.dma("sync", wpost[:, :], wpost_d[:, :], wpost, load=True)
            gw = kb.sb("gw_" + which, [128, 16], F32)
            kb.tt("vector", gw[:, :], self.modT[:, o + 32:o + 48], wpost[:, :], ALU.mult, [self.modT, wpost], [gw])
            res["gw"] = gw
        return res


def phase_L0(kb, mod_sh):
    kb.phase_begin()
    NJ = 24
    c_d = kb.din("c", [128, 16])
    w_d = kb.din("ada_w", [2048, NJ * 128])
    b_d = kb.din("ada_b", [128, NJ])
    cs = kb.sb("c_sb", [128, 16], F32)
    sc = kb.sb("silu_c", [128, 16], F32)
    bs = kb.sb("b_sb", [128, NJ], F32)
    os_ = kb.sb("o_sb", [128, NJ], F32)
    wf = [kb.sb("wf%d" % i, [128, 16, 128], F32) for i in range(3)]
    kb.dma("sync", cs[:, :], c_d[:, :], cs, load=True)
    kb.dma("sync", bs[:, :], b_d[:, :], bs, load=True)
    kb.act(sc[:, :], cs[:, :], AF.Silu, [cs], [sc])
    wv = w_d.rearrange("(k p) n -> p k n", p=128)
    for j in range(NJ):
        w = wf[j % 3]
        kb.dma("sync" if j % 2 == 0 else "gpsimd", w[:, :, :], wv[:, :, j * 128:(j + 1) * 128], w, load=True)
        ps = kb.next_ps()
        for kc in range(16):
            kb.mm(ps, ps[:, 0:1], w[:, kc, :], sc[:, kc:kc + 1], kc == 0, kc == 15, [w, sc])
        kb.tt("vector", os_[:, j:j + 1], ps[:, 0:1], bs[:, j:j + 1], ALU.add, [ps, bs], [os_])
    kb.dma("sync", mod_sh[:, :], os_[:, :], os_, load=False)


def phase_LA(kb, xsrc, ntok, ncols, modT_d, wpre_d, W, PT):
    kb.phase_begin()
    dn = Dense(kb)
    dn.consts()
    mc = dn.mod_consts(modT_d, wpre_d, None, "m")
    for t0 in range(0, ntok, T):
        dn.load_x(xsrc, t0)
        dn.prenorm(dn.R1, mc["gs"], mc["sh"], dn.R3)
        dn.proj(dn.R3, W, ncols, dn.epi_store(PT, t0))


def phase_LC(kb, ntok, layer, cfg):
    kb.phase_begin()
    dn = Dense(kb, n_wb=3, n_ost=2)
    dn.consts()
    dn.alloc_ffn()
    sfx = str(layer)
    modT = cfg["modT"]
    w_out = kb.din("w_out" + sfx, [2048, 2048])
    Wg = kb.din("ffn_gate" + sfx, [2048, FFN_H])
    Wu = kb.din("ffn_up" + sfx, [2048, FFN_H])
    Wd = kb.din("ffn_down" + sfx, [FFN_H, 2048])
    wpost_m = kb.din("wpost_m" + sfx, [128, 16])
    wpre_f = kb.din("wpre_f" + sfx, [128, 16])
    wpost_f = kb.din("wpost_f" + sfx, [128, 16])
    if layer == 0:
        glu_w = kb.din("glu_w", [1024, 1024])
        glu_b_d = kb.din("glu_b", [128, 8])
        gnw_d = kb.din("gnw", [128, 1])
        glu_b = kb.sb("glu_b", [128, 8], F32)
        gnw = kb.sb("gnw", [128, 1], F32)
        kb.dma("sync", glu_b[:, :], glu_b_d[:, :], glu_b, load=True)
        kb.dma("sync", gnw[:, :], gnw_d[:, :], gnw, load=True)
        wpre_m = kb.din("wpre_m0", [128, 16])
        w_z = kb.din("w_z", [2048, 1024])
        zT = cfg["zT"]
        zbufs = [kb.view("z_%d" % i, zT[i * 128:(i + 1) * 128, :]) for i in range(8)]
        mcm = dn.mod_consts(modT, wpre_m, wpost_m, "m")
    else:
        gnw_d = kb.din("gnw4", [128, 4])
        gnw4 = kb.sb("gnw4", [128, 4], F32)
        kb.dma("sync", gnw4[:, :], gnw_d[:, :], gnw4, load=True)
        mcm = dn.mod_consts(modT, None, wpost_m, "m")
    xout = cfg["xout"]
    xmidT = cfg["xmidT"]
    xmid_b = [kb.view("xmid_%d" % i, xmidT[i * 128:(i + 1) * 128, :]) for i in range(16)]
    mcf = dn.mod_consts(modT, wpre_f, wpost_f, "f")
    ncols_next = cfg.get("ncols_next", 0)
    if ncols_next:
        wpre_n = kb.din("wpre_n", [128, 16])
        w_in_n = kb.din("w_in_n", [2048, ncols_next])
        PTn = cfg["PTn"]
        mcn = dn.mod_consts(cfg["modT_n"], wpre_n, None, "m", tag="n")
    xsrc = cfg["xsrc"]
    for t0 in range(0, ntok, T):
        if layer == 0:
            dn.load_x(xsrc, t0)
            dn.prenorm(dn.R1, mcm["gs"], mcm["sh"], dn.R3)
            dn.proj(dn.R3, w_z, 1024, dn.epi_store(zT, t0, track=zbufs))
            dn.tail0(cfg["ysrc"], cfg["osrc"], lambda hd, t0_: zT[hd * 128:(hd + 1) * 128, t0_:t0_ + T], zbufs,
                     glu_w, glu_b, gnw, t0)
        else:
            dn.tail1(cfg["osrc"], cfg["rsrc"], gnw4, t0)
        dn.proj(dn.R3, w_out, D, dn.epi_to_R1())
        dn.postnorm_residual(mcm["gw"], lambda kc: xsrc(kc, t0), None,
                             lambda kc: xmidT[kc * 128:(kc + 1) * 128, t0:t0 + T], xmid_b)
        dn.prenorm(dn.R1, mcf["gs"], mcf["sh"], dn.R3)
        dn.ffn(Wg, Wu, Wd)
        dn.postnorm_residual(mcf["gw"], lambda kc: xmidT[kc * 128:(kc + 1) * 128, t0:t0 + T], xmid_b,
                             lambda kc: xout(kc, t0), None)
        if ncols_next:
            dn.prenorm(dn.R1, mcn["gs"], mcn["sh"], dn.R3)
            dn.proj(dn.R3, w_in_n, ncols_next, dn.epi_store(PTn, t0))


SC = 512
CH = 64


def role_psum(kb, names):
    return {n: kb.ps_list[i] for i, n in enumerate(names)}


class Rot:
    def __init__(self, items):
        self.items, self.i = items, 0

    def next(self):
        b = self.items[self.i % len(self.items)]
        self.i += 1
        return b


def phase_LD(kb, Ltot, qT, kT, vT, glT, oT):
    kb.phase_begin()
    P = role_psum(kb, ["o0", "o1", "S0", "S1", "a", "m0", "m1", "m2"])
    misc = Rot([P["m0"], P["m1"], P["m2"]])
    w2_d = kb.din("w2", [16, 256])
    ngb_d = kb.din("gb", [128, 2])
    ident_d = kb.din("ident", [128, 128])
    maskU_d = kb.din("maskU", [64, 512])
    rmask_d = kb.din("rmask", [128, 512])

    def const(name, shape, src):
        b = kb.sb(name, shape, F32)
        kb.dma("sync", b[:], src, b, load=True)
        return b
    w2 = const("w2", [16, 256], w2_d[:, :])
    gb = const("gb", [128, 2], ngb_d[:, :])
    ident = const("ident", [128, 128], ident_d[:, :])
    maskU = const("maskU", [64, 512], maskU_d[:, :])
    rmask = const("rmask", [128, 512], rmask_d[:, :])
    ngb = kb.sb("ngb", [128, 2], F32)
    kb.ts("vector", ngb[:, :], gb[:, :], -1.0, ALU.mult, [gb], [ngb])

    def dbl(name, shape, dtype, n=2):
        return Rot([kb.sb("%s%d" % (name, i), shape, dtype) for i in range(n)])
    qf = [dbl("qf%d" % kt, [128, SC], F32) for kt in range(2)]
    kf = [dbl("kf%d" % kt, [128, SC], F32) for kt in range(2)]
    vf = [dbl("vf%d" % kt, [128, SC], F32) for kt in range(2)]
    gl = dbl("gl", [16, SC], F32)
    e1 = [kb.sb("e1_%d" % kt, [128, SC], F32) for kt in range(2)]
    cum = [kb.sb("cum%d" % kt, [128, SC], F32) for kt in range(2)]
    eq = [dbl("eq%d" % kt, [128, SC], F32) for kt in range(2)]
    ek = [kb.sb("ek%d" % kt, [128, SC], F32) for kt in range(2)]
    dd = [kb.sb("dd%d" % kt, [128, SC], F32) for kt in range(2)]
    qt = [dbl("qt%d" % kt, [128, SC], BF16) for kt in range(2)]
    ktl = [dbl("kt%d" % kt, [128, SC], BF16) for kt in range(2)]
    kdT = [kb.sb("kdT%d" % kt, [128, SC], F32) for kt in range(2)]
    ktok = dbl("ktok", [64, 8, 256], BF16)
    vtok = dbl("vtok", [64, 8, 256], BF16)
    attnT = dbl("attnT", [64, SC], BF16)
    S = [kb.sb("S%d" % kt, [128, 256], F32) for kt in range(2)]
    Sb = [kb.sb("Sb%d" % kt, [128, 256], BF16) for kt in range(2)]
    ost = dbl("ost", [128, SC], F32, 4)
    for kt in range(2):
        kb.memset("vector", S[kt], S[kt][:, :], 0.0)
        kb.memset("vector", Sb[kt], Sb[kt][:, :], 0.0)
    scale_q = 256 ** -0.5
    nsc = Ltot // SC
    for s in range(nsc):
        ts_ = slice(s * SC, (s + 1) * SC)
        q_ = [qf[kt].next() for kt in range(2)]
        k_ = [kf[kt].next() for kt in range(2)]
        v_ = [vf[kt].next() for kt in range(2)]
        g_ = gl.next()
        kb.dma("sync", g_[:, :], glT[:, ts_], g_, load=True)
        for kt in range(2):
            kb.dma("sync", q_[kt][:, :], qT[kt * 128:(kt + 1) * 128, ts_], q_[kt], load=True)
            kb.dma("sync", k_[kt][:, :], kT[kt * 128:(kt + 1) * 128, ts_], k_[kt], load=True)
            kb.dma("gpsimd", v_[kt][:, :], vT[kt * 128:(kt + 1) * 128, ts_], v_[kt], load=True)
        eq_ = [eq[kt].next() for kt in range(2)]
        qt_ = [qt[kt].next() for kt in range(2)]
        kt_ = [ktl[kt].next() for kt in range(2)]
        for kt in range(2):
            pg = misc.next()
            kb.mm(pg, pg[:, :SC], w2[:, kt * 128:(kt + 1) * 128], g_[:, :], True, True, [w2, g_])
            kb.act(e1[kt][:, :], pg[:, :SC], AF.Exp, [pg, ngb], [e1[kt]], bias=ngb[:, kt:kt + 1], scale=-1.0)
            kb.act(e1[kt][:, :], e1[kt][:, :], AF.Ln, [e1[kt]], [e1[kt]], bias=1.0)
            kb.op("vector", lambda e, kt=kt: e.tensor_tensor_scan(out=cum[kt][:, :], data0=rmask[:, :], data1=e1[kt][:, :],
                                                                  initial=0.0, op0=ALU.mult, op1=ALU.add),
                  [rmask, e1[kt]], [cum[kt]])
            kb.act(eq_[kt][:, :], cum[kt][:, :], AF.Exp, [cum[kt]], [eq_[kt]], scale=-1.0 / 16)
            kb.act(ek[kt][:, :], cum[kt][:, :], AF.Exp, [cum[kt]], [ek[kt]], scale=1.0 / 16)
            c3 = cum[kt].ap.rearrange("p (c t) -> p c t", t=CH)
            d3 = dd[kt].ap.rearrange("p (c t) -> p c t", t=CH)
            kb.tt("vector", d3, c3, c3[:, :, CH - 1:CH].to_broadcast([128, SC // CH, CH]), ALU.subtract, [cum[kt]], [dd[kt]])
            kb.act(dd[kt][:, :], dd[kt][:, :], AF.Exp, [dd[kt]], [dd[kt]], scale=1.0 / 16)
            kb.stt(qt_[kt][:, :], q_[kt][:, :], scale_q, eq_[kt][:, :], ALU.mult, ALU.mult, [q_[kt], eq_[kt]], [qt_[kt]])
            kb.tt("vector", kt_[kt][:, :], k_[kt][:, :], ek[kt][:, :], ALU.mult, [k_[kt], ek[kt]], [kt_[kt]])
            kb.tt("gpsimd", kdT[kt][:, :], k_[kt][:, :], dd[kt][:, :], ALU.mult, [k_[kt], dd[kt]], [kdT[kt]])
        ktok_ = ktok.next()
        vtok_ = vtok.next()
        for (srcs, dst) in ((kdT, ktok_), (v_, vtok_)):
            for c2 in range(4):
                pt = misc.next()
                for cc in range(2):
                    n = c2 * 2 + cc
                    for kt in range(2):
                        kb.op("tensor", lambda e, pt=pt, cc=cc, kt=kt, n=n, srcs=srcs: e.transpose(
                            out=pt[:64, cc * 256 + kt * 128: cc * 256 + (kt + 1) * 128],
                            in_=srcs[kt][:, n * CH:(n + 1) * CH], identity=ident[:, :]), [srcs[kt], ident], [pt])
                kb.copy("scalar" if c2 % 2 else "vector", dst.ap[:, c2 * 2:c2 * 2 + 2, :],
                        pt[:64, :].rearrange("p (c d) -> p c d", d=256), [pt], [dst])
        pa = P["a"]
        for n in range(8):
            cs = slice(n * CH, (n + 1) * CH)
            for kt in range(2):
                kb.mm(pa, pa[:64, cs], kt_[kt][:, cs], qt_[kt][:, cs], kt == 0, kt == 1, [kt_[kt], qt_[kt]])
        at_ = attnT.next()
        kb.tt("vector", at_[:, :], pa[:64, :SC], maskU[:, :], ALU.mult, [pa, maskU], [at_])
        po = [P["o0"], P["o1"]]
        for n in range(8):
            cs = slice(n * CH, (n + 1) * CH)
            for dvt in range(2):
                ds_ = slice(dvt * 128, (dvt + 1) * 128)
                kb.mm(po[dvt], po[dvt][:, cs], vtok_[:, n, ds_], at_[:, cs], True, False, [vtok_, at_])
                for kt in range(2):
                    kb.mm(po[dvt], po[dvt][:, cs], Sb[kt][:, ds_], qt_[kt][:, cs], False, kt == 1, [Sb[kt], qt_[kt]])
            for kt in range(2):
                pS = P["S%d" % kt]
                kb.mm(pS, pS[:, :256], ktok_[:, n, kt * 128:(kt + 1) * 128], vtok_[:, n, :], True, True, [ktok_, vtok_])
                gl_col = eq_[kt][:, n * CH + CH - 1:n * CH + CH]
                kb.stt(Sb[kt][:, :], S[kt][:, :], gl_col, pS[:, :256], ALU.mult, ALU.add, [S[kt], eq_[kt], pS], [Sb[kt]])
                kb.stt(S[kt][:, :], S[kt][:, :], gl_col, pS[:, :256], ALU.mult, ALU.add, [S[kt], eq_[kt], pS], [S[kt]])
        for dvt in range(2):
            o_ = ost.next()
            kb.copy("scalar" if dvt else "vector", o_[:, :], po[dvt][:, :SC], [po[dvt]], [o_])
            kb.dma("sync", oT[dvt][:, ts_], o_[:, :], o_, load=False)


def gla_consts():
    ident = np.eye(128, dtype=np.float32)
    U = (np.arange(64)[:, None] <= np.arange(64)[None, :]).astype(np.float32)
    maskU = np.tile(U, (1, 8))
    rmask = np.ones((128, 512), np.float32)
    rmask[:, ::64] = 0.0
    return {"ident": ident, "maskU": maskU, "rmask": rmask}


TWO_PI = 6.283185
GELU_C = 0.7978845608028654


def sincos_turns(kb, fin, W, name, tmps, P_=128):
    if not tmps:
        tmps["ki"] = kb.sb("sc_ki", [P_, W], I32)
        tmps["kf"] = kb.sb("sc_kf", [P_, W], F32)
        tmps["f"] = kb.sb("sc_f", [P_, W], F32)
        tmps["g"] = kb.sb("sc_g", [P_, W], F32)
    ki, kf, f, g = tmps["ki"], tmps["kf"], tmps["f"], tmps["g"]
    sn = kb.sb(name + "_sin", [P_, W], F32)
    cs = kb.sb(name + "_cos", [P_, W], F32)
    kb.copy("vector", ki[:, :], fin[:, :], [fin], [ki])
    kb.copy("vector", kf[:, :], ki[:, :], [ki], [kf])
    kb.tt("vector", f[:, :], fin[:, :], kf[:, :], ALU.subtract, [fin, kf], [f])

    def wrap(x):
        kb.ts("vector", g[:, :], x[:, :], 0.5, ALU.is_gt, [x], [g])
        kb.tt("vector", x[:, :], x[:, :], g[:, :], ALU.subtract, [x, g], [x])
        kb.ts("vector", g[:, :], x[:, :], -0.5, ALU.is_lt, [x], [g])
        kb.tt("vector", x[:, :], x[:, :], g[:, :], ALU.add, [x, g], [x])
    wrap(f)
    kb.act(sn[:, :], f[:, :], AF.Sin, [f], [sn], scale=TWO_PI)
    kb.ts("vector", f[:, :], f[:, :], 0.25, ALU.add, [f], [f])
    wrap(f)
    kb.act(cs[:, :], f[:, :], AF.Sin, [f], [cs], scale=TWO_PI)
    return sn, cs


def phase_LS5(kb, Ltot, uT, yT):
    kb.phase_begin()
    P = role_psum(kb, ["re0", "im0", "re1", "im1", "y0", "y1", "x0", "x1"])

    def const(name, shape):
        d_ = kb.din(name, shape)
        b = kb.sb(name, shape, F32)
        kb.dma("sync", b[:], d_[:], b, load=True)
        return b
    lamre_c = const("lamre_c", [128, 4])
    lamim_c = const("lamim_c", [128, 4])
    lstep_c = const("lstep_c", [128, 4])
    lamre_r = const("lamre_r", [128, 512])
    lamim_r = const("lamim_r", [128, 512])
    lstep_r = const("lstep_r", [128, 512])
    Bre = const("BblkT_re", [128, 512])
    Bim = const("BblkT_im", [128, 512])
    Cre = const("CblkT_re", [128, 4, 128])
    Cim = const("CblkT_im", [128, 4, 128])
    dcol = const("dcol", [128, 1])
    tau1 = const("tau1", [128, 512])
    V = "vector"
    W = 512

    def new(name, shape=None):
        return kb.sb(name, shape or [128, W], F32)
    dt_c = new("dt_c", [128, 4])
    rho_c = new("rho_c", [128, 4])
    th_c = new("th_c", [128, 4])
    kb.act(dt_c[:, :], lstep_c[:, :], AF.Exp, [lstep_c], [dt_c])
    kb.tt(V, rho_c[:, :], lamre_c[:, :], dt_c[:, :], ALU.mult, [lamre_c, dt_c], [rho_c])
    kb.act(rho_c[:, :], rho_c[:, :], AF.Exp, [rho_c], [rho_c])
    kb.stt(th_c[:, :], lamim_c[:, :], 1.0 / (2 * np.pi), dt_c[:, :], ALU.mult, ALU.mult, [lamim_c, dt_c], [th_c])
    thk = kb.sb("thk", [128, 4], I32)
    thf = new("thf", [128, 4])
    kb.copy(V, thk[:, :], th_c[:, :], [th_c], [thk])
    kb.copy(V, thf[:, :], thk[:, :], [thk], [thf])
    kb.tt(V, th_c[:, :], th_c[:, :], thf[:, :], ALU.subtract, [th_c, thf], [th_c])
    cosT, sinT, rhoT = [], [], []
    sct = {}
    ang = new("ang")
    for st in range(4):
        kb.ts(V, ang[:, :], tau1[:, :], th_c[:, st:st + 1], ALU.mult, [tau1, th_c], [ang])
        sn, cs = sincos_turns(kb, ang, W, "tab%d" % st, sct)
        sinT.append(sn)
        cosT.append(cs)
        rt = new("rhoT%d" % st)
        kb.ts(V, rt[:, :], tau1[:, :], 0.0, ALU.mult, [tau1], [rt], s2=rho_c[:, st:st + 1], op1=ALU.add)
        rhoT.append(rt)
    dt_r = new("dt_r")
    mag = new("mag")
    thr = new("thr")
    kb.act(dt_r[:, :], lstep_r[:, :], AF.Exp, [lstep_r], [dt_r])
    kb.tt(V, mag[:, :], lamre_r[:, :], dt_r[:, :], ALU.mult, [lamre_r, dt_r], [mag])
    kb.act(mag[:, :], mag[:, :], AF.Exp, [mag], [mag])
    kb.stt(thr[:, :], lamim_r[:, :], 1.0 / (2 * np.pi), dt_r[:, :], ALU.mult, ALU.mult, [lamim_r, dt_r], [thr])
    sn_r, cs_r = sincos_turns(kb, thr, W, "row", sct)
    nr = new("nr")
    ni = new("ni")
    kb.tt(V, nr[:, :], mag[:, :], cs_r[:, :], ALU.mult, [mag, cs_r], [nr])
    kb.ts(V, nr[:, :], nr[:, :], -1.0, ALU.add, [nr], [nr])
    kb.tt(V, ni[:, :], mag[:, :], sn_r[:, :], ALU.mult, [mag, sn_r], [ni])
    den = new("den")
    t_a = new("t_a")
    t_b = new("t_b")
    kb.tt(V, den[:, :], lamre_r[:, :], lamre_r[:, :], ALU.mult, [lamre_r], [den])
    kb.tt(V, t_a[:, :], lamim_r[:, :], lamim_r[:, :], ALU.mult, [lamim_r], [t_a])
    kb.tt(V, den[:, :], den[:, :], t_a[:, :], ALU.add, [den, t_a], [den])
    kb.op(V, lambda e: e.reciprocal(out=den[:, :], in_=den[:, :]), [den], [den])
    fre = new("fre")
    fim = new("fim")
    kb.tt(V, t_a[:, :], nr[:, :], lamre_r[:, :], ALU.mult, [nr, lamre_r], [t_a])
    kb.tt(V, t_b[:, :], ni[:, :], lamim_r[:, :], ALU.mult, [ni, lamim_r], [t_b])
    kb.tt(V, fre[:, :], t_a[:, :], t_b[:, :], ALU.add, [t_a, t_b], [fre])
    kb.tt(V, fre[:, :], fre[:, :], den[:, :], ALU.mult, [fre, den], [fre])
    kb.tt(V, t_a[:, :], ni[:, :], lamre_r[:, :], ALU.mult, [ni, lamre_r], [t_a])
    kb.tt(V, t_b[:, :], nr[:, :], lamim_r[:, :], ALU.mult, [nr, lamim_r], [t_b])
    kb.tt(V, fim[:, :], t_a[:, :], t_b[:, :], ALU.subtract, [t_a, t_b], [fim])
    kb.tt(V, fim[:, :], fim[:, :], den[:, :], ALU.mult, [fim, den], [fim])
    BbT_re = new("BbT_re")
    BbT_im = new("BbT_im")
    kb.tt(V, t_a[:, :], fre[:, :], Bre[:, :], ALU.mult, [fre, Bre], [t_a])
    kb.tt(V, t_b[:, :], fim[:, :], Bim[:, :], ALU.mult, [fim, Bim], [t_b])
    kb.tt(V, BbT_re[:, :], t_a[:, :], t_b[:, :], ALU.subtract, [t_a, t_b], [BbT_re])
    kb.tt(V, t_a[:, :], fre[:, :], Bim[:, :], ALU.mult, [fre, Bim], [t_a])
    kb.tt(V, t_b[:, :], fim[:, :], Bre[:, :], ALU.mult, [fim, Bre], [t_b])
    kb.tt(V, BbT_im[:, :], t_a[:, :], t_b[:, :], ALU.add, [t_a, t_b], [BbT_im])
    nCim = kb.sb("nCim", [128, 4, 128], F32)
    kb.ts(V, nCim[:, :, :], Cim[:, :, :], -1.0, ALU.mult, [Cim], [nCim])

    def dbl(name, n=2):
        return Rot([new("%s%d" % (name, i)) for i in range(n)])
    u_r = dbl("u_", 3)
    t1r, t2r, t3r, t4r = dbl("t1"), dbl("t2"), dbl("t3"), dbl("t4")
    brer, bimr = dbl("bre"), dbl("bim")
    zrer, zimr = dbl("zre"), dbl("zim")
    m1r, m2r, m3r, m4r = dbl("m1", 1), dbl("m2", 1), dbl("m3", 1), dbl("m4", 1)
    xre_r = [dbl("xre%d" % st) for st in range(4)]
    xim_r = [dbl("xim%d" % st) for st in range(4)]
    ypre = dbl("ypre")
    x2 = dbl("x2")
    ost = dbl("ost", 3)
    prev = [None] * 4
    psre = Rot([P["re0"], P["re1"]])
    psim = Rot([P["im0"], P["im1"]])
    psy = Rot([P["y0"], P["y1"]])
    G = "gpsimd"
    for s in range(Ltot // W):
        ts_ = slice(s * W, (s + 1) * W)
        u_ = u_r.next()
        kb.dma("sync", u_[:, :], uT[:, ts_], u_, load=True)
        py = psy.next()
        for st in range(4):
            pre, pim = psre.next(), psim.next()
            ss_ = slice(st * 128, (st + 1) * 128)
            kb.mm(pre, pre[:, :W], BbT_re[:, ss_], u_[:, :], True, True, [BbT_re, u_])
            kb.mm(pim, pim[:, :W], BbT_im[:, ss_], u_[:, :], True, True, [BbT_im, u_])
            t1, t2, t3, t4 = t1r.next(), t2r.next(), t3r.next(), t4r.next()
            bre, bim = brer.next(), bimr.next()
            kb.tt(V, t1[:, :], cosT[st][:, :], pre[:, :W], ALU.mult, [cosT[st], pre], [t1])
            kb.tt(V, t2[:, :], sinT[st][:, :], pim[:, :W], ALU.mult, [sinT[st], pim], [t2])
            kb.tt(G, bre[:, :], t1[:, :], t2[:, :], ALU.add, [t1, t2], [bre])
            kb.tt(V, t3[:, :], cosT[st][:, :], pim[:, :W], ALU.mult, [cosT[st], pim], [t3])
            kb.tt(V, t4[:, :], sinT[st][:, :], pre[:, :W], ALU.mult, [sinT[st], pre], [t4])
            kb.tt(G, bim[:, :], t3[:, :], t4[:, :], ALU.subtract, [t3, t4], [bim])
            zre, zim = zrer.next(), zimr.next()
            xre, xim = xre_r[st].next(), xim_r[st].next()
            if prev[st] is None:
                ire, iim, rdeps = 0.0, 0.0, []
            else:
                ire, iim = prev[st][0][:, W - 1:W], prev[st][1][:, W - 1:W]
                rdeps = [prev[st][0], prev[st][1]]
            kb.op(V, lambda e, zre=zre, bre=bre, ire=ire, st=st: e.tensor_tensor_scan(
                out=zre[:, :], data0=rhoT[st][:, :], data1=bre[:, :], initial=ire, op0=ALU.mult, op1=ALU.add),
                [rhoT[st], bre] + rdeps, [zre])
            kb.op(V, lambda e, zim=zim, bim=bim, iim=iim, st=st: e.tensor_tensor_scan(
                out=zim[:, :], data0=rhoT[st][:, :], data1=bim[:, :], initial=iim, op0=ALU.mult, op1=ALU.add),
                [rhoT[st], bim] + rdeps, [zim])
            m1, m2, m3, m4 = m1r.next(), m2r.next(), m3r.next(), m4r.next()
            kb.tt(G, m1[:, :], cosT[st][:, :], zre[:, :], ALU.mult, [cosT[st], zre], [m1])
            kb.tt(G, m2[:, :], sinT[st][:, :], zim[:, :], ALU.mult, [sinT[st], zim], [m2])
            kb.tt(V, xre[:, :], m1[:, :], m2[:, :], ALU.subtract, [m1, m2], [xre])
            kb.tt(G, m3[:, :], sinT[st][:, :], zre[:, :], ALU.mult, [sinT[st], zre], [m3])
            kb.tt(G, m4[:, :], cosT[st][:, :], zim[:, :], ALU.mult, [cosT[st], zim], [m4])
            kb.tt(V, xim[:, :], m3[:, :], m4[:, :], ALU.add, [m3, m4], [xim])
            prev[st] = (xre, xim)
            kb.mm(py, py[:, :W], Cre[:, st, :], xre[:, :], st == 0, False, [Cre, xre])
            kb.mm(py, py[:, :W], nCim[:, st, :], xim[:, :], False, st == 3, [nCim, xim])
        yp = ypre.next()
        kb.stt(yp[:, :], u_[:, :], dcol[:, 0:1], py[:, :W], ALU.mult, ALU.add, [u_, dcol, py], [yp])
        a_ = x2.next()
        kb.act(a_[:, :], yp[:, :], AF.Square, [yp], [a_])
        kb.ts(V, a_[:, :], a_[:, :], 0.044715, ALU.mult, [a_], [a_], s2=1.0, op1=ALU.add)
        kb.tt(G, a_[:, :], a_[:, :], yp[:, :], ALU.mult, [a_, yp], [a_])
        kb.act(a_[:, :], a_[:, :], AF.Sigmoid, [a_], [a_], scale=2.0 * GELU_C)
        o_ = ost.next()
        kb.tt(G, o_[:, :], a_[:, :], yp[:, :], ALU.mult, [a_, yp], [o_])
        kb.dma("sync", yT[:, ts_], o_[:, :], o_, load=False)


def s5_host_params(inp, core):
    g0 = core * 8
    lam_re = np.asarray(inp["s5_lambda_re"], np.float32)[g0:g0 + 8]
    lam_im = np.asarray(inp["s5_lambda_im"], np.float32)[g0:g0 + 8]
    lstep = np.asarray(inp["s5_log_step"], np.float32)[g0:g0 + 8]
    b_re = np.asarray(inp["s5_b_re"], np.float32)[g0:g0 + 8]
    b_im = np.asarray(inp["s5_b_im"], np.float32)[g0:g0 + 8]
    c_re = np.asarray(inp["s5_c_re"], np.float32)[g0:g0 + 8]
    c_im = np.asarray(inp["s5_c_im"], np.float32)[g0:g0 + 8]
    d = np.asarray(inp["s5_d"], np.float32)[g0:g0 + 8]
    row = lambda a: np.ascontiguousarray(np.broadcast_to(a.reshape(1, 512), (128, 512)))
    col = lambda a: np.ascontiguousarray(a.reshape(4, 128).T)
    ls_full = np.repeat(lstep[:, None], 64, axis=1)
    out = {"lamre_c": col(lam_re), "lamim_c": col(lam_im), "lstep_c": col(ls_full),
           "lamre_r": row(lam_re), "lamim_r": row(lam_im), "lstep_r": row(ls_full)}
    Bre = np.zeros((128, 512), np.float32)
    Bim = np.zeros((128, 512), np.float32)
    Cre = np.zeros((128, 4, 128), np.float32)
    Cim = np.zeros((128, 4, 128), np.float32)
    for g in range(8):
        Bre[g * 16:(g + 1) * 16, g * 64:(g + 1) * 64] = b_re[g].T
        Bim[g * 16:(g + 1) * 16, g * 64:(g + 1) * 64] = b_im[g].T
        st, h = g // 2, g % 2
        Cre[h * 64:(h + 1) * 64, st, g * 16:(g + 1) * 16] = c_re[g].T
        Cim[h * 64:(h + 1) * 64, st, g * 16:(g + 1) * 16] = c_im[g].T
    out.update({"BblkT_re": Bre, "BblkT_im": Bim, "CblkT_re": Cre, "CblkT_im": Cim,
                "dcol": np.ascontiguousarray(d.reshape(128, 1)),
                "tau1": np.ascontiguousarray(np.broadcast_to(np.arange(1, 513, dtype=np.float32)[None, :], (128, 512)))})
    return out


NEG_BIG = -30000.0


def phase_LG(kb, Ltot, qrT, krT, vrT, a_d, b_d, oT, debug=False, nst=6):
    kb.phase_begin()
    P = role_psum(kb, ["o0", "o1", "v", "S", "m0", "m1", "m2", "m3"])
    misc = Rot([P["m0"], P["m1"], P["m2"], P["m3"]])
    po_r = Rot([P["o0"], P["o1"]])
    V, G = "vector", "gpsimd"
    W = SC
    NCH = W // CH

    def const(name, shape):
        d_ = kb.din(name, shape)
        b = kb.sb(name, shape, F32)
        kb.dma("sync", b[:], d_[:], b, load=True)
        return b
    cw = [const("cw_q", [128, 4]), const("cw_k", [128, 4]), const("cw_v", [128, 4])]
    alog = const("alog", [1, 1])
    dtb = const("dtb", [1, 1])
    ident = const("ident", [128, 128])
    mLs = const("mLs", [64, W])
    mUs = const("mUs", [64, W])
    mUi = const("mUi", [64, W])
    I8 = const("I8", [64, W])
    rmask = const("rmask_row", [1, W])
    ones_row = const("ones_row", [1, 128])
    ones_f = kb.sb("ones_f", [128, 128], F32)
    kb.memset(V, ones_f, ones_f[:, :], 1.0)
    epsb = kb.sb("epsb", [128, 1], F32)
    kb.memset(V, epsb, epsb[:, :], EPS)
    nA = kb.sb("nA", [1, 1], F32)
    kb.act(nA[:, :], alog[:, :], AF.Exp, [alog], [nA])
    kb.ts(V, nA[:, :], nA[:, :], -1.0, ALU.mult, [nA], [nA])

    def new(name, shape, dtype=F32):
        return kb.sb(name, shape, dtype)

    def dbl(name, shape, dtype=F32, n=2):
        return Rot([kb.sb("%s%d" % (name, i), shape, dtype) for i in range(n)])
    xin = [dbl("xin%d" % i, [128, W + 3]) for i in range(3)]
    a_r = dbl("a_", [1, W])
    b_r = dbl("b_", [1, W])
    acc = [new("acc%d" % i, [128, W]) for i in range(3)]
    sq = new("sq", [128, W])
    rs = [new("rs%d" % i, [128, W]) for i in range(2)]
    qn = new("qn", [128, W])
    kn = new("kn", [128, W])
    knb = new("knb", [128, W], BF16)
    qnb = new("qnb", [128, W], BF16)
    qd = dbl("qd", [128, W], BF16)
    egc = dbl("egc", [128, W])
    rows = {n_: new("row_" + n_, [1, W]) for n_ in ("e", "g", "gc", "l2", "gcl", "ngc", "beta", "bg", "ed", "egc")}
    cols = dbl("cols", [64, 3 * NCH])
    vb = dbl("vb", [64, NCH, 128], BF16)
    kbg = dbl("kbg", [64, NCH, 128], BF16)
    kdec = dbl("kdec", [64, NCH, 128], BF16)
    M1 = new("M1", [64, W])
    M1T = new("M1T", [64, W])
    DT = new("DT", [64, W])
    Pm = [new("Pm%d" % i, [64, W]) for i in range(2)]
    PmT = [new("PmT%d" % i, [64, W]) for i in range(2)]
    RT = [new("RT%d" % i, [64, W]) for i in range(2)]
    attnT = dbl("attnT", [64, W], BF16)
    TTb = dbl("TTb", [64, W], BF16)
    nwT = dbl("nwT", [128, W], BF16)
    S = new("S", [128, 128])
    Sb = new("Sb", [128, 128], BF16)
    kb.memset(V, S, S[:, :], 0.0)
    kb.memset(V, Sb, Sb[:, :], 0.0)
    vnew = dbl("vnew", [64, 128], BF16, 3)
    ost = dbl("ost", [128, W], F32, 2)
    srcT = [qrT, krT, vrT]

    for s in range(Ltot // W):
        ts_ = slice(s * W, (s + 1) * W)
        x_ = [xin[i].next() for i in range(3)]
        for i in range(3):
            if s == 0:
                kb.memset(V, x_[i], x_[i][:, 0:3], 0.0)
                kb.dma("sync", x_[i][:, 3:W + 3], srcT[i][:, 0:W], x_[i], load=True)
            else:
                kb.dma("sync", x_[i][:, :], srcT[i][:, s * W - 3:(s + 1) * W], x_[i], load=True)
        a_ = a_r.next()
        b_ = b_r.next()
        kb.dma("sync", a_[:, :], a_d[:, ts_], a_, load=True)
        kb.dma("sync", b_[:, :], b_d[:, ts_], b_, load=True)
        for i in range(3):
            kb.ts(V, acc[i][:, :], x_[i][:, 0:W], cw[i][:, 0:1], ALU.mult, [x_[i], cw[i]], [acc[i]])
            for j in range(1, 4):
                kb.stt(acc[i][:, :], x_[i][:, j:j + W], cw[i][:, j:j + 1], acc[i][:, :], ALU.mult, ALU.add,
                       [x_[i], cw[i], acc[i]], [acc[i]])
            kb.act(acc[i][:, :], acc[i][:, :], AF.Silu, [acc[i]], [acc[i]])
        qc, kc, vc = acc
        for i, (src, dst, scl) in enumerate(((qc, qn, 128 ** -0.5), (kc, kn, 1.0))):
            kb.act(sq[:, :], src[:, :], AF.Square, [src], [sq])
            pss = misc.next()
            kb.mm(pss, pss[:, :W], ones_f[:, :], sq[:, :], True, True, [ones_f, sq])
            kb.act(rs[i][:, :], pss[:, :W], AF.Sqrt, [pss, epsb], [rs[i]], bias=epsb[:, 0:1])
            kb.op(V, lambda e, i=i: e.reciprocal(out=rs[i][:, :], in_=rs[i][:, :]), [rs[i]], [rs[i]])
            kb.stt(dst[:, :], src[:, :], scl, rs[i][:, :], ALU.mult, ALU.mult, [src, rs[i]], [dst])
        kb.copy(G, knb[:, :], kn[:, :], [kn], [knb])
        kb.copy(G, qnb[:, :], qn[:, :], [qn], [qnb])
        R_ = rows
        kb.act(R_["e"][:, :], a_[:, :], AF.Exp, [a_, dtb], [R_["e"]], bias=dtb[0:1, 0:1])
        kb.act(R_["e"][:, :], R_["e"][:, :], AF.Ln, [R_["e"]], [R_["e"]], bias=1.0)
        kb.ts(V, R_["g"][:, :], R_["e"][:, :], nA[0:1, 0:1], ALU.mult, [R_["e"], nA], [R_["g"]])
        kb.op(V, lambda e: e.tensor_tensor_scan(out=R_["gc"][:, :], data0=rmask[:, :], data1=R_["g"][:, :], initial=0.0,
                                                op0=ALU.mult, op1=ALU.add), [rmask, R_["g"]], [R_["gc"]])
        kb.act(R_["l2"][:, :], b_[:, :], AF.Exp, [b_], [R_["l2"]], scale=-1.0)
        kb.act(R_["l2"][:, :], R_["l2"][:, :], AF.Ln, [R_["l2"]], [R_["l2"]], bias=1.0)
        kb.tt(V, R_["gcl"][:, :], R_["gc"][:, :], R_["l2"][:, :], ALU.subtract, [R_["gc"], R_["l2"]], [R_["gcl"]])
        kb.ts(V, R_["ngc"][:, :], R_["gc"][:, :], -1.0, ALU.mult, [R_["gc"]], [R_["ngc"]])
        kb.act(R_["beta"][:, :], R_["l2"][:, :], AF.Exp, [R_["l2"]], [R_["beta"]], scale=-1.0)
        kb.act(R_["bg"][:, :], R_["gcl"][:, :], AF.Exp, [R_["gcl"]], [R_["bg"]])
        g3 = R_["gc"].ap.rearrange("p (c t) -> p c t", t=CH)
        e3 = R_["ed"].ap.rearrange("p (c t) -> p c t", t=CH)
        kb.tt(V, e3, g3[:, :, CH - 1:CH].to_broadcast([1, NCH, CH]), g3, ALU.subtract, [R_["gc"]], [R_["ed"]])
        kb.act(R_["ed"][:, :], R_["ed"][:, :], AF.Exp, [R_["ed"]], [R_["ed"]])
        kb.act(R_["egc"][:, :], R_["gc"][:, :], AF.Exp, [R_["gc"]], [R_["egc"]])
        pc = misc.next()
        for qi, rn in enumerate(("beta", "bg", "ed")):
            for n in range(NCH):
                kb.mm(pc, pc[:64, qi * NCH + n:qi * NCH + n + 1], R_[rn][0:1, n * CH:(n + 1) * CH], ones_row[0:1, 0:1],
                      True, True, [R_[rn], ones_row])
        cols_ = cols.next()
        kb.copy(V, cols_[:, :], pc[:64, :3 * NCH], [pc], [cols_])
        pb = misc.next()
        kb.mm(pb, pb[:, :W], ones_row[0:1, :], R_["egc"][0:1, :], True, True, [ones_row, R_["egc"]])
        egc_ = egc.next()
        kb.copy("scalar", egc_[:, :], pb[:, :W], [pb], [egc_])
        qd_ = qd.next()
        kb.tt(V, qd_[:, :], qn[:, :], egc_[:, :], ALU.mult, [qn, egc_], [qd_])
        vb_, kbg_, kdec_ = vb.next(), kbg.next(), kdec.next()
        for (src, outs) in ((kn, ((kbg_, 1), (kdec_, 2))), (vc, ((vb_, 0),))):
            for hb in range(2):
                pt = misc.next()
                for cc in range(4):
                    n = hb * 4 + cc
                    kb.op("tensor", lambda e, pt=pt, cc=cc, n=n, src=src: e.transpose(
                        out=pt[:64, cc * 128:(cc + 1) * 128], in_=src[:, n * CH:(n + 1) * CH], identity=ident[:, :]),
                        [src, ident], [pt])
                for (dst, qi) in outs:
                    cb = cols_.ap[:, qi * NCH + hb * 4:qi * NCH + hb * 4 + 4].unsqueeze(2).to_broadcast([64, 4, 128])
                    kb.tt(V, dst.ap[:, hb * 4:hb * 4 + 4, :], pt[:64, :].rearrange("p (c d) -> p c d", d=128), cb,
                          ALU.mult, [pt, cols_], [dst])
        pG, pQK, pE1, pE1T = misc.next(), misc.next(), misc.next(), misc.next()
        for n in range(NCH):
            cs = slice(n * CH, (n + 1) * CH)
            kb.mm(pG, pG[:64, cs], knb[:, cs], knb[:, cs], True, True, [knb])
            kb.mm(pQK, pQK[:64, cs], knb[:, cs], qnb[:, cs], True, True, [knb, qnb])
            kb.mm(pE1, pE1[:64, cs], R_["gcl"][0:1, cs], ones_row[0:1, 0:CH], True, False, [R_["gcl"], ones_row])
            kb.mm(pE1, pE1[:64, cs], ones_row[0:1, 0:CH], R_["ngc"][0:1, cs], False, True, [R_["ngc"], ones_row])
            kb.mm(pE1T, pE1T[:64, cs], ones_row[0:1, 0:CH], R_["gcl"][0:1, cs], True, False, [R_["gcl"], ones_row])
            kb.mm(pE1T, pE1T[:64, cs], R_["ngc"][0:1, cs], ones_row[0:1, 0:CH], False, True, [R_["ngc"], ones_row])
        kb.tt(V, M1[:, :], pE1[:64, :W], mLs[:, :], ALU.min, [pE1, mLs], [M1])
        kb.act(M1[:, :], M1[:, :], AF.Exp, [M1], [M1])
        kb.tt(V, M1T[:, :], pE1T[:64, :W], mUs[:, :], ALU.min, [pE1T, mUs], [M1T])
        kb.act(M1T[:, :], M1T[:, :], AF.Exp, [M1T], [M1T])
        kb.stt(Pm[0][:, :], pG[:64, :W], -1.0, M1[:, :], ALU.mult, ALU.mult, [pG, M1], [Pm[0]])
        kb.stt(PmT[0][:, :], pG[:64, :W], -1.0, M1T[:, :], ALU.mult, ALU.mult, [pG, M1T], [PmT[0]])
        kb.tt(G, RT[0][:, :], PmT[0][:, :], I8[:, :], ALU.add, [PmT[0], I8], [RT[0]])
        pE2T = misc.next()
        for n in range(NCH):
            cs = slice(n * CH, (n + 1) * CH)
            kb.mm(pE2T, pE2T[:64, cs], ones_row[0:1, 0:CH], R_["gc"][0:1, cs], True, False, [R_["gc"], ones_row])
            kb.mm(pE2T, pE2T[:64, cs], R_["ngc"][0:1, cs], ones_row[0:1, 0:CH], False, True, [R_["ngc"], ones_row])
        kb.tt(V, DT[:, :], pE2T[:64, :W], mUi[:, :], ALU.min, [pE2T, mUi], [DT])
        kb.act(DT[:, :], DT[:, :], AF.Exp, [DT], [DT])
        at_ = attnT.next()
        kb.tt(V, at_[:, :], pQK[:64, :W], DT[:, :], ALU.mult, [pQK, DT], [at_])
        cur = 0
        for j in range(1, nst):
            nxt = 1 - cur
            pP = misc.next()
            for n in range(NCH):
                cs = slice(n * CH, (n + 1) * CH)
                kb.mm(pP, pP[:64, cs], PmT[cur][:, cs], Pm[cur][:, cs], True, True, [PmT[cur], Pm[cur]])
            if j < 5:
                pPT = misc.next()
                for n in range(NCH):
                    cs = slice(n * CH, (n + 1) * CH)
                    kb.mm(pPT, pPT[:64, cs], Pm[cur][:, cs], PmT[cur][:, cs], True, True, [PmT[cur], Pm[cur]])
            kb.copy(V, Pm[nxt][:, :], pP[:64, :W], [pP], [Pm[nxt]])
            if j < 5:
                kb.copy("scalar", PmT[nxt][:, :], pPT[:64, :W], [pPT], [PmT[nxt]])
            pR = misc.next()
            for n in range(NCH):
                cs = slice(n * CH, (n + 1) * CH)
                kb.mm(pR, pR[:64, cs], Pm[nxt][:, cs], RT[cur][:, cs], True, True, [Pm[nxt], RT[cur]])
            kb.tt(V, RT[nxt][:, :], RT[cur][:, :], pR[:64, :W], ALU.add, [RT[cur], pR], [RT[nxt]])
            cur = nxt
        TT_ = TTb.next()
        kb.copy(V, TT_[:, :], RT[cur][:, :], [RT[cur]], [TT_])
        pW = misc.next()
        for n in range(NCH):
            cs = slice(n * CH, (n + 1) * CH)
            kb.mm(pW, pW[:, cs], kbg_[:, n, :], TT_[:, cs], True, True, [kbg_, TT_])
        nw_ = nwT.next()
        kb.act(nw_[:, :], pW[:, :W], AF.Copy, [pW], [nw_], scale=-1.0)
        po = po_r.next()
        pv, pS = P["v"], P["S"]
        for n in range(NCH):
            cs = slice(n * CH, (n + 1) * CH)
            kb.mm(pv, pv[:64, :128], TT_[:, cs], vb_[:, n, :], True, False, [TT_, vb_])
            kb.mm(pv, pv[:64, :128], nw_[:, cs], Sb[:, :], False, True, [nw_, Sb])
            vn = vnew.next()
            kb.copy("scalar", vn[:, :], pv[:64, :128], [pv], [vn])
            kb.mm(po, po[:, cs], Sb[:, :], qd_[:, cs], True, False, [Sb, qd_])
            kb.mm(po, po[:, cs], vn[:, :], at_[:, cs], False, True, [vn, at_])
            kb.mm(pS, pS[:, :128], kdec_[:, n, :], vn[:, :], True, True, [kdec_, vn])
            gcol = egc_[:, n * CH + CH - 1:n * CH + CH]
            kb.stt(Sb[:, :], S[:, :], gcol, pS[:, :128], ALU.mult, ALU.add, [S, egc_, pS], [Sb])
            kb.stt(S[:, :], S[:, :], gcol, pS[:, :128], ALU.mult, ALU.add, [S, egc_, pS], [S])
        o_ = ost.next()
        kb.copy(V, o_[:, :], po[:, :W], [po], [o_])
        kb.dma("sync", oT[:, ts_], o_[:, :], o_, load=False)
        if debug and s == 0:
            for nm, bf in (("qn", qn), ("kn", kn), ("vc", vc), ("gc", R_["gc"]), ("beta", R_["beta"]), ("M1", M1), ("M1T", M1T),
                           ("DT", DT), ("TT", RT[cur]), ("cols", cols_), ("egc", egc_), ("Pm0", Pm[0]), ("Pm1", Pm[1]), ("PmT0", PmT[0]), ("PmT1", PmT[1]), ("RT0", RT[0]), ("RT1", RT[1])):
                shp = [int(x) for x in bf.ap.shape]
                dd_ = kb.dout("dbg_" + nm, shp)
                kb.dma("sync", dd_[:], bf[:], bf, load=False)


def gdn_consts():
    i = np.arange(64)
    Ls = np.where(i[:, None] > i[None, :], 0.0, NEG_BIG).astype(np.float32)
    Us = np.where(i[:, None] < i[None, :], 0.0, NEG_BIG).astype(np.float32)
    Ui = np.where(i[:, None] <= i[None, :], 0.0, NEG_BIG).astype(np.float32)
    rm = np.ones((1, 512), np.float32)
    rm[:, ::64] = 0.0
    return {"ident": np.eye(128, dtype=np.float32), "mLs": np.tile(Ls, (1, 8)), "mUs": np.tile(Us, (1, 8)),
            "mUi": np.tile(Ui, (1, 8)), "I8": np.tile(np.eye(64, dtype=np.float32), (1, 8)), "rmask_row": rm,
            "ones_row": np.ones((1, 128), np.float32)}


def chunk_layout(v):
    v = np.asarray(v, np.float32).reshape(-1, 128)
    return np.ascontiguousarray(v.T)


def run(nc, in_maps):
    res = run_bass_kernel_spmd(nc, in_maps, core_ids=list(range(NCORES)))
    return res.results


def build_fused(Ltot):
    NT = Ltot // NCORES
    kb = KB()
    kb.tok_per_core = NT
    xT_full = kb.din("xT_full", [2048, Ltot])
    xT_loc = kb.din("xT_loc", [2048, NT])
    outT = kb.dout("outT", [2048, NT])
    ds_ = kb.dscratch
    mod_sh, mod_all = ds_("mod_sh", [128, 24]), ds_("mod_all", [1024, 24])
    PTc = ds_("PTc", [514, Ltot])
    Yc = [ds_("Yc%d" % i, [128, Ltot]) for i in range(2)]
    Yall = [ds_("Yall%d" % i, [1024, Ltot]) for i in range(2)]
    Yloc = [ds_("Yloc%d" % i, [1024, NT]) for i in range(2)]
    zT, xmid0, rT = ds_("zT", [1024, NT]), ds_("xmid0", [2048, NT]), ds_("rT", [2048, NT])
    x1T = [ds_("x1T%d" % i, [1024, NT]) for i in range(2)]
    X1all = [ds_("X1all%d" % i, [NCORES * 1024, NT]) for i in range(2)]
    PTd = ds_("PTd", [784, Ltot])
    Oc = [ds_("Oc%d" % i, [128, Ltot]) for i in range(2)]
    Oall = [ds_("Oall%d" % i, [1024, Ltot]) for i in range(2)]
    Oloc = [ds_("Oloc%d" % i, [1024, NT]) for i in range(2)]
    xmid1 = ds_("xmid1", [2048, NT])
    G8 = [list(range(NCORES))]

    def ag(a, b):
        return lambda g: g.collective_compute("AllGather", ALU.bypass, replica_groups=G8, ins=[a.ap().opt()],
                                              outs=[b.ap().opt()])
    phase_L0(kb, mod_sh.ap())
    kb.phase_end(ag(mod_sh, mod_all))
    modv = mod_all.ap().rearrange("(r p) j -> p r j", p=128)
    mod0, mod1 = modv[:, 0:4, :], modv[:, 4:8, :]
    phase_LA(kb, lambda kc, t0: xT_full[kc * 128:(kc + 1) * 128, t0:t0 + T], Ltot, 514, mod0,
             kb.din("wpre_m0", [128, 16]), kb.din("w_in0c", [2048, 514]), PTc.ap())
    kb.phase_end()
    P_ = PTc.ap()
    phase_LS5(kb, Ltot, P_[0:128, :], Yc[0].ap())
    kb.phase_end(ag(Yc[0], Yall[0]), wait=False)
    phase_LG(kb, Ltot, P_[128:256, :], P_[256:384, :], P_[384:512, :], P_[512:513, :], P_[513:514, :], Yc[1].ap())
    kb.phase_end(ag(Yc[1], Yall[1]))
    kb.phase_begin()
    for i in range(2):
        kb.dram_copy_dyn("yloc%d" % i, Yloc[i], lambda tokoff, i=i: Yall[i].ap()[:, bass.ds(tokoff, NT)])
    Ya, Yb = Yloc[0].ap(), Yloc[1].ap()
    x1a = [x1T[0].ap(), x1T[1].ap()]

    def x1loc(kc, t0):
        return x1a[kc // 8][(kc % 8) * 128:(kc % 8 + 1) * 128, t0:t0 + T]
    cfg0 = dict(modT=mod0, zT=zT.ap(), xout=x1loc, xmidT=xmid0.ap(), ncols_next=2048, PTn=rT.ap(), modT_n=mod1,
                xsrc=lambda kc, t0: xT_loc[kc * 128:(kc + 1) * 128, t0:t0 + T],
                ysrc=lambda kc, t0: Ya[kc * 128:(kc + 1) * 128, t0:t0 + T],
                osrc=lambda hd, t0: Yb[hd * 128:(hd + 1) * 128, t0:t0 + T])
    phase_LC(kb, NT, 0, cfg0)
    kb.phase_end()
    kb.phase_end([ag(x1T[0], X1all[0]), ag(x1T[1], X1all[1])])
    X1 = [X1all[0].ap(), X1all[1].ap()]

    def x1src(kc, t0):
        r, tl = t0 // NT, t0 % NT
        return X1[kc // 8][r * 1024 + (kc % 8) * 128:r * 1024 + (kc % 8 + 1) * 128, tl:tl + T]
    phase_LA(kb, x1src, Ltot, 784, mod1, kb.din("wpre_m1", [128, 16]), kb.din("w_in1c", [2048, 784]), PTd.ap())
    kb.phase_end()
    Pd = PTd.ap()
    phase_LD(kb, Ltot, Pd[0:256, :], Pd[256:512, :], Pd[512:768, :], Pd[768:784, :], [Oc[0].ap(), Oc[1].ap()])
    kb.phase_end([ag(Oc[0], Oall[0]), ag(Oc[1], Oall[1])])
    kb.phase_begin()
    for i in range(2):
        kb.dram_copy_dyn("oloc%d" % i, Oloc[i], lambda tokoff, i=i: Oall[i].ap()[:, bass.ds(tokoff, NT)])
    Ol, r_ = [Oloc[0].ap(), Oloc[1].ap()], rT.ap()
    cfg1 = dict(modT=mod1, xout=lambda kc, t0: outT[kc * 128:(kc + 1) * 128, t0:t0 + T], xmidT=xmid1.ap(), xsrc=x1loc,
                osrc=lambda c, t0: Ol[c % 2][(c // 2) * 128:(c // 2 + 1) * 128, t0:t0 + T],
                rsrc=lambda c, t0: r_[c * 128:(c + 1) * 128, t0:t0 + T])
    phase_LC(kb, NT, 1, cfg1)
    kb.phase_end()
    kb.phase_end()
    names = sorted(kb._din.keys())
    return kb.build(), names


def _c(a):
    return np.ascontiguousarray(a, dtype=np.float32)


def _run_fused(inp, Ltot):
    inp = {k: np.asarray(v) for k, v in inp.items()}
    NT = Ltot // NCORES
    cl = chunk_layout
    xT = _c(inp["x"][0].T)
    ada_w = np.concatenate([inp["ada_w0"], inp["ada_w1"]], axis=1)
    ada_b = np.concatenate([inp["ada_b0"], inp["ada_b1"]])
    w0, w1 = inp["w_in0"], inp["w_in1"]
    cwT = _c(inp["gdn_conv_w"].T)
    shared = {"xT_full": xT, "c": cl(inp["c"][0]), "wpre_m0": cl(inp["mix_pre0"]), "wpre_m1": cl(inp["mix_pre1"]),
              "wpre_n": cl(inp["mix_pre1"]), "w_z": _c(w0[:, 3072:4096]), "w_in_n": _c(w1[:, 4096:6144]),
              "glu_w": _c(inp["s5_glu_w"]), "glu_b": cl(inp["s5_glu_b"]), "gnw": _c(inp["gdn_norm_w"].reshape(128, 1)),
              "gnw4": cl(inp["gla_norm_w"])}
    for l in (0, 1):
        s_ = str(l)
        shared.update({"w_out" + s_: _c(inp["w_out" + s_]), "ffn_gate" + s_: _c(inp["ffn_gate" + s_]),
                       "ffn_up" + s_: _c(inp["ffn_up" + s_]), "ffn_down" + s_: _c(inp["ffn_down" + s_]),
                       "wpost_m" + s_: cl(inp["mix_post" + s_]), "wpre_f" + s_: cl(inp["ffn_pre" + s_]),
                       "wpost_f" + s_: cl(inp["ffn_post" + s_])})
    shared.update(gdn_consts())
    shared.update(gla_consts())
    nc, names = build_fused(Ltot)
    in_maps = []
    for i in range(NCORES):
        h, hg = i // 2, i // 2
        d = dict(shared)
        d["xT_loc"] = _c(xT[:, i * NT:(i + 1) * NT])
        d["ada_w"] = _c(ada_w[:, i * 3072:(i + 1) * 3072])
        d["ada_b"] = cl(ada_b[i * 3072:(i + 1) * 3072])
        d["w_in0c"] = _c(np.concatenate([w0[:, i * 128:(i + 1) * 128], w0[:, 1024 + h * 128:1024 + (h + 1) * 128],
                                         w0[:, 1536 + h * 128:1536 + (h + 1) * 128], w0[:, 2048 + i * 128:2048 + (i + 1) * 128],
                                         w0[:, 4096 + i:4097 + i], w0[:, 4104 + i:4105 + i]], axis=1))
        d["w_in1c"] = _c(np.concatenate([w1[:, hg * 256:(hg + 1) * 256], w1[:, 1024 + hg * 256:1024 + (hg + 1) * 256],
                                         w1[:, 2048 + i * 256:2048 + (i + 1) * 256], w1[:, 6144:6160]], axis=1))
        d.update(s5_host_params(inp, i))
        d.update({"cw_q": _c(cwT[h * 128:(h + 1) * 128]), "cw_k": _c(cwT[512 + h * 128:512 + (h + 1) * 128]),
                  "cw_v": _c(cwT[1024 + i * 128:1024 + (i + 1) * 128]),
                  "alog": _c(inp["gdn_a_log"][i].reshape(1, 1)), "dtb": _c(inp["gdn_dt_bias"][i].reshape(1, 1)),
                  "w2": _c(inp["gla_gate_w2"][:, hg * 256:(hg + 1) * 256]),
                  "gb": cl(inp["gla_gate_b"][hg * 256:(hg + 1) * 256])})
        missing = [n for n in names if n not in d]
        assert not missing, missing
        in_maps.append({n: d[n] for n in names})
    res = run(nc, in_maps)
    outT = np.concatenate([r["outT"] for r in res], axis=1)
    return np.ascontiguousarray(outT.T)[None].astype(np.float32)


def kernel(**inp):
    return _run_fused(inp, L)
```
